# Optimizing a Trainium2 kernel written in Bass

```python
import jax, jax.numpy as jnp
from jax import lax
import numpy as np

D_MODEL = 1024
BATCH = 16
SEQ = 4096
DEPTH = 1
DEC_BATCH = 32
DEC_SEQ = 16
PAST_LEN = 1024

CHUNK = 64
N_HEADS = 16
HEAD_DIM = D_MODEL // N_HEADS
D_ATTN = N_HEADS * HEAD_DIM
D_CONV = D_MODEL
CONV_WIDTH = 3
D_FF = 4 * D_MODEL
Q_BLOCK = 128
RMS_EPS = 1e-6
N_PROJ = 3 * D_CONV + 3 * D_ATTN + 2 * D_MODEL
_SPLIT = [D_CONV, 2 * D_CONV, 3 * D_CONV,
          3 * D_CONV + D_ATTN, 3 * D_CONV + 2 * D_ATTN, 3 * D_CONV + 3 * D_ATTN,
          3 * D_CONV + 3 * D_ATTN + D_MODEL]

kernel_name = 'gated_conv_stickbreak_hybrid_step'


def _rmsnorm(x, g):
    x32 = x.astype(jnp.float32)
    y = x32 * lax.rsqrt(jnp.mean(x32 * x32, axis=-1, keepdims=True) + RMS_EPS)
    return (y * g.astype(jnp.float32)).astype(x.dtype)


def _project(xn, w_in):
    z = jnp.einsum('btd,dn->btn', xn, w_in)
    b_g, c_g, h, q, k, v, g_conv, g_attn = jnp.split(z, _SPLIT, axis=-1)
    bsz, t = xn.shape[0], xn.shape[1]
    q = q.reshape(bsz, t, N_HEADS, HEAD_DIM)
    k = k.reshape(bsz, t, N_HEADS, HEAD_DIM)
    v = v.reshape(bsz, t, N_HEADS, HEAD_DIM)
    return b_g, c_g, h, q, k, v, g_conv, g_attn


def _causal_conv(u_ext, conv_w, t):
    out = u_ext[:, 0:t] * conv_w[0]
    for i in range(1, CONV_WIDTH):
        out = out + u_ext[:, i:i + t] * conv_w[i]
    return out


def _stick_breaking(q, k, v, q_pos, k_pos):
    z = jnp.einsum('bqhd,bkhd->bhqk', q.astype(jnp.float32), k.astype(jnp.float32)) * (HEAD_DIM ** -0.5)
    mask = (k_pos[None, :] < q_pos[:, None])[None, None]
    log_rest = jnp.where(mask, jax.nn.log_sigmoid(-z), 0.0)
    suffix = lax.cumsum(log_rest, axis=3, reverse=True) - log_rest
    a = jnp.where(mask, jnp.exp(jax.nn.log_sigmoid(z) + suffix), 0.0)
    o = jnp.einsum('bhqk,bkhd->bqhd', a, v.astype(jnp.float32))
    return o.astype(v.dtype)


def _prompt_attention(q, k, v):
    bsz, s = q.shape[0], q.shape[1]
    nb = s // Q_BLOCK
    qb = jnp.moveaxis(q.reshape(bsz, nb, Q_BLOCK, N_HEADS, HEAD_DIM), 1, 0)
    pos = jnp.arange(s, dtype=jnp.int32)
    pb = pos.reshape(nb, Q_BLOCK)
    out = lax.map(lambda a: _stick_breaking(a[0], k, v, a[1], pos), (qb, pb))
    return jnp.moveaxis(out, 0, 1).reshape(bsz, s, N_HEADS, HEAD_DIM)


def _merge_ffn(x, conv_out, attn_out, g_conv, g_attn, w_out, g_ffn, w_up, w_down):
    bsz, t = x.shape[0], x.shape[1]
    mixed = jax.nn.sigmoid(g_conv) * conv_out + jax.nn.sigmoid(g_attn) * attn_out.reshape(bsz, t, D_ATTN)
    h = x + jnp.einsum('btc,cd->btd', mixed, w_out)
    up = jnp.einsum('btd,df->btf', _rmsnorm(h, g_ffn), w_up)
    return h + jnp.einsum('btf,fd->btd', jnp.square(jax.nn.relu(up)), w_down)


def setup_inputs(seed: int = 0) -> dict:
    key = jax.random.key(seed)
    ks = jax.random.split(key, 14)
    f32 = jnp.float32
    return {
        'x_prompt': jax.random.normal(ks[0], (BATCH, SEQ, D_MODEL), f32),
        'x_sample': jax.random.normal(ks[1], (DEC_BATCH, DEC_SEQ, D_MODEL), f32),
        'cache_conv': jax.random.normal(ks[2], (DEPTH, DEC_BATCH, CONV_WIDTH - 1, D_CONV), f32),
        'cache_k': jax.random.normal(ks[3], (DEPTH, DEC_BATCH, PAST_LEN, N_HEADS, HEAD_DIM), f32),
        'cache_v': jax.random.normal(ks[4], (DEPTH, DEC_BATCH, PAST_LEN, N_HEADS, HEAD_DIM), f32),
        'g_mix': 1.0 + 0.02 * jax.random.normal(ks[5], (DEPTH, D_MODEL), f32),
        'w_in': jax.random.normal(ks[6], (DEPTH, D_MODEL, N_PROJ), f32) * D_MODEL ** -0.5,
        'conv_w': jax.random.normal(ks[7], (DEPTH, CONV_WIDTH, D_CONV), f32) * CONV_WIDTH ** -0.5,
        'w_out': jax.random.normal(ks[8], (DEPTH, D_MODEL, D_MODEL), f32) * D_MODEL ** -0.5,
        'g_ffn': 1.0 + 0.02 * jax.random.normal(ks[9], (DEPTH, D_MODEL), f32),
        'w_up': jax.random.normal(ks[10], (DEPTH, D_MODEL, D_FF), f32) * D_MODEL ** -0.5,
        'w_down': jax.random.normal(ks[11], (DEPTH, D_FF, D_MODEL), f32) * D_FF ** -0.5,
        'g_final': 1.0 + 0.02 * jax.random.normal(ks[12], (D_MODEL,), f32),
    }


def reference(x_prompt, x_sample, cache_conv, cache_k, cache_v, g_mix, w_in, conv_w, w_out,
              g_ffn, w_up, w_down, g_final):
    xp, xs = x_prompt, x_sample
    seq = xp.shape[1]
    dec_seq = xs.shape[1]
    past = cache_k.shape[2]
    conv_p, k_p, v_p, conv_s, k_s, v_s = [], [], [], [], [], []
    for l in range(DEPTH):
        b_g, c_g, hc, q, k, v, g_c, g_a = _project(_rmsnorm(xp, g_mix[l]), w_in[l])
        u_ext = jnp.concatenate([jnp.zeros((xp.shape[0], CONV_WIDTH - 1, D_CONV), xp.dtype), c_g * hc], axis=1)
        conv_out = b_g * _causal_conv(u_ext, conv_w[l], seq)
        attn_out = _prompt_attention(q, k, v)
        conv_p.append(u_ext[:, -(CONV_WIDTH - 1):])
        k_p.append(k)
        v_p.append(v)
        xp = _merge_ffn(xp, conv_out, attn_out, g_c, g_a, w_out[l], g_ffn[l], w_up[l], w_down[l])
        b_g, c_g, hc, q, k, v, g_c, g_a = _project(_rmsnorm(xs, g_mix[l]), w_in[l])
        u_ext = jnp.concatenate([cache_conv[l].astype(xs.dtype), c_g * hc], axis=1)
        conv_out = b_g * _causal_conv(u_ext, conv_w[l], dec_seq)
        k_all = jnp.concatenate([cache_k[l].astype(k.dtype), k], axis=1)
        v_all = jnp.concatenate([cache_v[l].astype(v.dtype), v], axis=1)
        q_pos = past + jnp.arange(dec_seq, dtype=jnp.int32)
        k_pos = jnp.arange(past + dec_seq, dtype=jnp.int32)
        attn_out = _stick_breaking(q, k_all, v_all, q_pos, k_pos)
        conv_s.append(u_ext[:, -(CONV_WIDTH - 1):])
        k_s.append(k)
        v_s.append(v)
        xs = _merge_ffn(xs, conv_out, attn_out, g_c, g_a, w_out[l], g_ffn[l], w_up[l], w_down[l])
    y_prompt = _rmsnorm(xp, g_final)
    y_sample = _rmsnorm(xs, g_final)
    return (y_prompt, y_sample, jnp.stack(conv_p), jnp.stack(k_p), jnp.stack(v_p),
            jnp.stack(conv_s), jnp.stack(k_s), jnp.stack(v_s))
```

```python
import contextlib
import numpy as np
import concourse.bass as bass
import concourse.mybir as mybir
from concourse.bass_utils import run_bass_kernel_spmd

F32 = mybir.dt.float32
BF16 = mybir.dt.bfloat16
AF = mybir.ActivationFunctionType
ALU = mybir.AluOpType
AX = mybir.AxisListType

D = 1024
NH = 16
HD = 64
DFF = 4096
NPROJ = 8192
DEC = 16
EPS = 1e-6
NEG = -30000.0
N_CORES = 8
import os
KSTAGE = float(os.environ.get('KSTAGE', '99'))
KATT = int(os.environ.get('KATT', '99'))
KSKIP = set(os.environ.get('KSKIP', '').split(','))
EPOCH = 30000


class Buf:
    __slots__ = ("name", "w", "rs", "excl")

    def __init__(self, name="", excl=False):
        self.name = name
        self.w = None
        self.rs = []
        self.excl = excl


class Op:
    __slots__ = ("eng", "fn", "deps", "dma", "sem", "val", "signal", "k")

    def __init__(self, eng, fn, dma):
        self.eng = eng
        self.fn = fn
        self.dma = dma
        self.deps = ()
        self.sem = None
        self.val = 0
        self.signal = False
        self.k = -1


class Sched:
    ENGS = ("pe", "act", "dve", "pool", "sp")

    def __init__(self, nc, es, n_dma_sems=10, n_epochs=5):
        self.nc = nc
        self.ops = {e: [] for e in self.ENGS}
        self.pending = {e: set() for e in self.ENGS}
        self.last = {e: None for e in self.ENGS}
        self.esems = {e: [es.enter_context(nc.semaphore(f"s_{e}{j}")) for j in range(n_epochs)]
                      for e in ("pe", "act", "dve", "pool")}
        self.dsems = {q: [es.enter_context(nc.semaphore(f"d_{q}{j}")) for j in range(n_dma_sems)]
                      for q in ("sp", "pool")}
        self.dcnt = {q: [0] * n_dma_sems for q in ("sp", "pool")}
        self.dlast = {q: [None] * n_dma_sems for q in ("sp", "pool")}
        self.drr = {q: 0 for q in ("sp", "pool")}
        self.all_dma = []

    def _deps(self, eng, r, w, is_dma):
        raw = set()
        other = set()
        for b in r:
            if b.w is not None:
                raw.add(b.w)
            if b.excl:
                other.update(b.rs)
        for b in w:
            if b.w is not None:
                other.add(b.w)
            other.update(b.rs)
        deps = set()
        for d in raw | other:
            if d.dma:
                deps.add(d)
            elif d.eng != eng:
                deps.add(d)
            else:
                if is_dma or eng != "pe":
                    deps.add(d)
        return deps

    def _record(self, o, r, w):
        deps = self._deps(o.eng, r, w, o.dma)
        if self.pending[o.eng]:
            deps |= self.pending[o.eng]
            self.pending[o.eng] = set()
        o.deps = tuple(deps)
        for d in deps:
            d.signal = True
        self.ops[o.eng].append(o)
        self.last[o.eng] = o
        for b in w:
            b.w = o
            b.rs = []
        for b in r:
            if b.w is not o:
                b.rs.append(o)
        return o

    def op(self, eng, fn, r=(), w=()):
        return self._record(Op(eng, fn, False), r, w)

    def dma(self, q, out, in_, r=(), w=(), **kw):
        if os.environ.get("KQ_SP", "0") == "1":
            q = "sp"
        o = Op(q, (lambda e, out=out, in_=in_, kw=kw: e.dma_start(out=out, in_=in_, **kw)), True)
        j = self.drr[q]
        self.drr[q] = (j + 1) % len(self.dsems[q])
        prev = self.dlast[q][j]
        self.dcnt[q][j] += 1
        o.sem = self.dsems[q][j]
        o.val = 16 * self.dcnt[q][j]
        self._record(o, r, w)
        if prev is not None:
            o.deps = o.deps + (prev,)
        self.dlast[q][j] = o
        self.all_dma.append(o)
        return o

    def barrier(self):
        lasts = [self.last[e] for e in ("pe", "act", "dve", "pool") if self.last[e] is not None]
        for d in lasts:
            d.signal = True
        dm = [x for q in self.dlast for x in self.dlast[q] if x is not None]
        for e in self.ENGS:
            self.pending[e] |= set(lasts) | set(dm)

    def mm(self, out, lhsT, rhs, start, stop, r, w):
        return self.op("pe", lambda e: e.matmul(out, lhsT=lhsT, rhs=rhs, start=start, stop=stop,
                                                skip_group_check=True), r, w)

    def tr(self, out, in_, ident, r, w):
        return self.op("pe", lambda e: e.transpose(out, in_, ident), r, w)

    def act(self, out, in_, func, r, w, bias=None, scale=None, accum=None):
        kw = {}
        if bias is not None:
            kw["bias"] = bias
        if scale is not None:
            kw["scale"] = scale
        if accum is not None:
            kw["accum_out"] = accum
        return self.op("act", lambda e: e.activation(out=out, in_=in_, func=func, **kw), r, w)

    def tt(self, eng, out, in0, in1, op, r, w):
        return self.op(eng, lambda e: e.tensor_tensor(out=out, in0=in0, in1=in1, op=op), r, w)

    def ts(self, eng, out, in0, s1, op0, r, w, s2=None, op1=None):
        if op1 is None:
            return self.op(eng, lambda e: e.tensor_scalar(out=out, in0=in0, scalar1=s1, scalar2=None,
                                                          op0=op0), r, w)
        return self.op(eng, lambda e: e.tensor_scalar(out=out, in0=in0, scalar1=s1, scalar2=s2,
                                                      op0=op0, op1=op1), r, w)

    def stt(self, eng, out, in0, scalar, in1, op0, op1, r, w):
        return self.op(eng, lambda e: e.scalar_tensor_tensor(out=out, in0=in0, scalar=scalar, in1=in1,
                                                             op0=op0, op1=op1), r, w)

    def cp(self, eng, out, in_, r, w):
        if eng == "act":
            return self.op("act", lambda e: e.activation(out=out, in_=in_, func=AF.Identity), r, w)
        return self.op(eng, lambda e: e.tensor_copy(out=out, in_=in_), r, w)

    def memset(self, eng, ap, val, w):
        return self.op(eng, lambda e: e.memset(ap, val), (), w)

    def recip(self, out, in_, r, w):
        return self.op("dve", lambda e: e.reciprocal(out=out, in_=in_), r, w)

    def emit(self):
        nc = self.nc
        for e in ("pe", "act", "dve", "pool"):
            k = 0
            for o in self.ops[e]:
                if not o.dma and o.signal:
                    o.k = k
                    o.sem = self.esems[e][k // EPOCH]
                    o.val = (k % EPOCH) + 1
                    k += 1
            assert k <= EPOCH * len(self.esems[e]), (e, k)

        def run(ename, eng):
            waited = {}
            for o in self.ops[ename]:
                need = {}
                for d in o.deps:
                    key = id(d.sem)
                    if key not in need or need[key][1] < d.val:
                        need[key] = (d.sem, d.val)
                for key, (sem, val) in need.items():
                    if waited.get(key, 0) >= val:
                        continue
                    eng.wait_ge(sem, val)
                    waited[key] = val
                ins = o.fn(eng)
                if o.dma:
                    ins.then_inc(o.sem, 16)
                elif o.signal:
                    ins.then_inc(o.sem, 1)
            if ename in self.dsems:
                for j, sem in enumerate(self.dsems[ename]):
                    if self.dcnt[ename][j] > 0:
                        eng.wait_ge(sem, 16 * self.dcnt[ename][j])

        with nc.Block() as block:
            @block.tensor
            def _(eng):
                run("pe", eng)

            @block.scalar
            def _(eng):
                run("act", eng)

            @block.vector
            def _(eng):
                run("dve", eng)

            @block.gpsimd
            def _(eng):
                run("pool", eng)

            @block.sync
            def _(eng):
                run("sp", eng)


class Tile:
    __slots__ = ("t", "buf", "bufs")

    def __init__(self, t, name, nb=0):
        self.t = t
        self.buf = Buf(name)
        self.bufs = [Buf(f"{name}{i}") for i in range(nb)]


class Ring:
    def __init__(self, tiles):
        self.tiles = tiles
        self.i = 0

    def next(self):
        t = self.tiles[self.i]
        self.i = (self.i + 1) % len(self.tiles)
        return t


def build_nc(NB, S, SB, PAST):
    nc = bass.Bass("TRN2", target_bir_lowering=False)
    NBLK = S // 512
    NS = SB * DEC
    NKB_S = PAST // 128

    def din(name, shape, dt=F32):
        return nc.dram_tensor(name, list(shape), dt, kind="ExternalInput").ap()

    def dout(name, shape, dt=F32):
        return nc.dram_tensor(name, list(shape), dt, kind="ExternalOutput").ap()

    xp = din("xp", [NB, S, D])
    xsm = din("xsm", [NS, D])
    cconv = din("cconv", [SB * 2, D])
    ck = din("ck", [SB, PAST, D])
    cv = din("cv", [SB, PAST, D])
    g_mix = din("g_mix", [D])
    w_in = din("w_in", [D, NPROJ])
    conv_w = din("conv_w", [3, D])
    w_out = din("w_out", [D, D])
    g_ffn = din("g_ffn", [D])
    w_up = din("w_up", [D, DFF])
    w_down = din("w_down", [DFF, D])
    g_final = din("g_final", [D])
    consts = din("consts", [128, 640])

    yp = dout("yp", [NB, S, D])
    ys = dout("ys", [NS, D])
    ncp = dout("ncp", [NB * 2, D])
    nkp = dout("nkp", [NB, S, D])
    nvp = dout("nvp", [NB, S, D])
    ncs = dout("ncs", [SB * 2, D])
    nks = dout("nks", [NS, D])
    nvs = dout("nvs", [NS, D])

    NWB = 34
    wsc = nc.dram_tensor("wsc", [NWB, 128, 4096], BF16).ap()
    kts = nc.dram_tensor("kts", [NB, 8, 128, S], BF16).ap()
    vsc = nc.dram_tensor("vsc", [NB, S, D], BF16).ap()
    WB_IN, WB_OUT, WB_UP, WB_DN = 0, 16, 18, 26

    es = contextlib.ExitStack()
    with es:
        S_ = Sched(nc, es)

        def sb(name, shape, dt, nb=0):
            return Tile(es.enter_context(nc.sbuf_tensor(name, list(shape), dt)), name, nb)

        def ring(name, shape, dt, n):
            return Ring([sb(f"{name}{i}", shape, dt) for i in range(n)])

        ps = es.enter_context(nc.psum_tensor("ps", [128, 4096], F32))
        PB = [Buf(f"bank{k}", excl=True) for k in range(8)]

        def bank(k):
            return ps[:, k * 512:(k + 1) * 512]

        gb_i = [0]
        GB = [[5, 6, 7]]

        def set_gb(lst):
            GB[0] = list(lst)
            gb_i[0] = 0

        def gbank():
            lst = GB[0]
            k = lst[gb_i[0] % len(lst)]
            gb_i[0] = (gb_i[0] + 1) % len(lst)
            return k

        cst32 = sb("cst32", [128, 640], F32)
        ident_f = cst32.t[:, 512:640]
        cbf = sb("cbf", [128, 512], BF16)
        ident_b = cbf.t[:, 0:128]
        ntri_b = cbf.t[:, 128:256]
        nones_b = cbf.t[:, 256:384]
        m0_b = cbf.t[:, 384:512]
        zer_b = sb("zer_b", [128, 64], BF16)
        cw = sb("cw", [128, 8, 3], F32)
        gmx = sb("gmx", [128, 8], F32)
        gff = sb("gff", [128, 8], F32)
        gfin = sb("gfin", [128, D], F32)
        epsb = sb("epsb", [128, 1], F32)
        oneb = sb("oneb", [128, 1], F32)

        xs = sb("xs", [128, 4, D], F32, nb=8)
        xnb = sb("xnb", [128, 4, D], BF16, nb=4)
        xnT = sb("xnT", [128, 8, 512], BF16, nb=8)
        mixT = sb("mixT", [128, 8, 512], BF16, nb=8)
        actT = sb("actT", [128, 32, 512], BF16, nb=32)
        stat = sb("stat", [128, 16], F32)
        wsl = ring("wsl", [128, 8, 512], BF16, int(os.environ.get("KWSL", "3")))
        wck = ring("wck", [128, 8, 128], BF16, 7)
        stg = ring("stg", [128, 512], F32, 2)
        vbf = ring("vbf", [128, 512], BF16, 2)
        ktmp = ring("ktmp", [128, 512], BF16, 2)
        QTr = ring("QT", [128, 512], BF16, 2)
        sgar = ring("sga", [128, 512], F32, 2)
        cvpr = ring("cvp", [128, 512], F32, 2)
        cgs = sb("cgs", [128, 512], F32)
        sgc = sb("sgc", [128, 512], F32)
        ut = ring("ut", [128, 514], F32, 2)
        tA = sb("tA", [128, 512], F32)
        tB = sb("tB", [128, 512], F32)
        carry = sb("carry", [128, 8, 2], F32, nb=8)
        ktp = ring("ktp", [128, 512], BF16, 3)
        vtp = ring("vtp", [128, 4, 128], BF16, 3)
        er = ring("e", [128, 2, 512], F32, 2)
        spr = ring("sp", [128, 2, 512], BF16, 3)
        ar = ring("a", [128, 2, 512], BF16, 2)
        R32 = sb("R32", [128, 2, 512], F32)
        rbr = ring("Rb", [128, 2, 512], BF16, 3)
        mtmp = sb("mtmp", [128, 512], F32)
        rtmp = ring("rtmp", [128, 512], BF16, 2)

        QTs = sb("QTs", [128, 8, NS], BF16)
        KTn = sb("KTn", [128, 8, NS], BF16)
        sgas = sb("sgas", [128, 8, NS], F32)
        cvps = sb("cvps", [128, 8, NS], F32)
        us = sb("us", [128, SB, 18], F32)
        carrs = sb("carrs", [128, 8, SB * 2], F32)
        cct = sb("cct", [SB * 2, D], F32)
        ccT = sb("ccT", [128, 8, SB * 2], F32)
        vnew = sb("vnew", [16, SB, D], BF16, nb=SB)
        zer2 = sb("zer2", [128, 256], BF16)

        if os.environ.get('KDBG'):
            print('SBUF remaining bytes/partition:', nc.sbuf_bytes_remaining)
        KTS = [[[Buf() for _ in range(NBLK)] for _ in range(8)] for _ in range(NB)]
        VSB = [[[Buf() for _ in range(2)] for _ in range(S // 128)] for _ in range(NB)]
        WSC = [Buf() for _ in range(NWB)]

        S_.dma("sp", cst32.t[:], consts[:, :], w=[cst32.buf])
        S_.cp("dve", cbf.t[:], cst32.t[:, 0:512], r=[cst32.buf], w=[cbf.buf])
        S_.memset("pool", zer_b.t[:], 0.0, w=[zer_b.buf])
        S_.memset("pool", zer2.t[:], 0.0, w=[zer2.buf])
        S_.memset("pool", epsb.t[:], EPS, w=[epsb.buf])
        S_.memset("pool", oneb.t[:], 1.0, w=[oneb.buf])
        for tap in range(3):
            S_.dma("sp", cw.t[:, :, tap], conv_w[tap].rearrange("(c p) -> p c", p=128), w=[cw.buf],
                   allow_slow_non_contiguous=True)
        S_.dma("sp", gmx.t[:], g_mix.rearrange("(c p) -> p c", p=128), w=[gmx.buf],
               allow_slow_non_contiguous=True)
        S_.dma("sp", gff.t[:], g_ffn.rearrange("(c p) -> p c", p=128), w=[gff.buf],
               allow_slow_non_contiguous=True)
        S_.dma("sp", gfin.t[:], g_final.partition_broadcast(128), w=[gfin.buf])

        stage_views = [
            (xs.t[:].rearrange("p a d -> p (a d)").rearrange("p (c n) -> p c n", c=8), [xs.buf] + xs.bufs),
            (actT.t[:, 0:16, :].rearrange("p a n -> p (a n)").bitcast(F32).rearrange("p (c n) -> p c n", c=8),
             actT.bufs[0:16]),
            (actT.t[:, 16:32, :].rearrange("p a n -> p (a n)").bitcast(F32).rearrange("p (c n) -> p c n", c=8),
             actT.bufs[16:32]),
        ]
        wblocks = []
        for j in range(16):
            wblocks.append((WB_IN + j, w_in, 0, j * 512, gmx))
        for j in range(2):
            wblocks.append((WB_OUT + j, w_out, 0, j * 512, None))
        for j in range(8):
            wblocks.append((WB_UP + j, w_up, 0, j * 512, gff))
        for g in range(4):
            for hf in range(2):
                wblocks.append((WB_DN + g * 2 + hf, w_down, g * 1024, hf * 512, None))
        for n, (bi, W, r0, n0, gain) in enumerate(wblocks):
            sv, sbufs = stage_views[n % 3]
            S_.dma("sp", sv, W[r0:r0 + 1024, n0:n0 + 512].rearrange("(c p) n -> p c n", p=128), w=sbufs)
            wt = wsl.next()
            eng = "dve" if n % 2 == 0 else "act"
            if gain is None:
                S_.cp(eng, wt.t[:], sv, r=sbufs, w=[wt.buf])
            else:
                for c in range(8):
                    if eng == "dve":
                        S_.ts(eng, wt.t[:, c, :], sv[:, c, :], gain.t[:, c:c + 1], ALU.mult,
                              r=sbufs + [gain.buf], w=[wt.buf])
                    else:
                        S_.act(wt.t[:, c, :], sv[:, c, :], AF.Identity, r=sbufs + [gain.buf], w=[wt.buf],
                               scale=gain.t[:, c:c + 1])
            S_.dma("sp", wsc[bi].rearrange("p (c n) -> p c n", c=8), wt.t[:], r=[wt.buf], w=[WSC[bi]])

        def rms_stats(src_ap_fn, nsub, npart, rbufs, col0):
            S_.memset("pool", stat.t[:, col0:col0 + nsub], 0.0, w=[stat.buf])
            for sub in range(nsub):
                S_.act(xnb.t[0:npart, sub, :], src_ap_fn(sub), AF.Square, r=rbufs(sub) + [stat.buf],
                       w=[xnb.bufs[sub], stat.buf], accum=stat.t[0:npart, col0 + sub:col0 + sub + 1])
            sl = stat.t[0:npart, col0:col0 + nsub]
            S_.act(sl, sl, AF.Ln, r=[stat.buf, epsb.buf], w=[stat.buf], bias=epsb.t[0:npart, :], scale=1.0 / D)
            S_.act(sl, sl, AF.Exp, r=[stat.buf], w=[stat.buf], scale=-0.5)

        def normalize_transpose(nsub, npart, col0):
            ntok = nsub * npart
            for sub in range(nsub):
                S_.ts("dve", xnb.t[0:npart, sub, :], xs.t[0:npart, sub, :],
                      stat.t[0:npart, col0 + sub:col0 + sub + 1], ALU.mult,
                      r=[xs.bufs[sub * 2], xs.bufs[sub * 2 + 1], stat.buf], w=[xnb.bufs[sub]])
            for dc in range(8):
                k = gbank()
                pT = bank(k).bitcast(BF16)
                for sub in range(nsub):
                    S_.tr(pT[:, sub * npart:(sub + 1) * npart], xnb.t[0:npart, sub, dc * 128:(dc + 1) * 128],
                          ident_b[0:npart, 0:npart], r=[xnb.bufs[sub], cbf.buf], w=[PB[k]])
                if os.environ.get("KT_NOCOPY") == "1":
                    continue
                S_.cp("act" if (dc % 2 == 1 and os.environ.get("KT_ACT") == "1") else "dve",
                      xnT.t[:, dc, 0:ntok], pT[:, 0:ntok], r=[PB[k]], w=[xnT.bufs[dc]])

        def load_wblock(bi):
            wt = wsl.next()
            S_.dma("sp", wt.t[:], wsc[bi].rearrange("p (c n) -> p c n", c=8), r=[WSC[bi]], w=[wt.buf])
            return wt

        def load_wchunk(kind, c):
            bi = WB_IN + kind * 2 + c // 4
            wt = wck.next()
            src = wsc[bi].rearrange("p (c n) -> p c n", c=8)[:, :, (c % 4) * 128:(c % 4) * 128 + 128]
            S_.dma("sp", wt.t[:], src, r=[WSC[bi]], w=[wt.buf])
            return wt

        def proj_fm(wt, ntok):
            k = gbank()
            for dc in range(8):
                S_.mm(bank(k)[:, 0:ntok], wt.t[:, dc, :], xnT.t[:, dc, 0:ntok], dc == 0, dc == 7,
                      r=[wt.buf, xnT.bufs[dc]], w=[PB[k]])
            return k

        xnT_all = list(xnT.bufs)

        def ffn_and_out(nsub, npart, y_dst_fn):
            ntok = nsub * npart
            set_gb([4, 5, 6, 7])
            for hf in range(2):
                wt = load_wblock(WB_OUT + hf)
                for sub in range(nsub):
                    k = gbank()
                    for c in range(8):
                        S_.mm(bank(k)[0:npart, :], mixT.t[:, c, sub * npart:(sub + 1) * npart], wt.t[:, c, :],
                              c == 0, c == 7, r=[mixT.bufs[c], wt.buf], w=[PB[k]])
                    hb = xs.bufs[sub * 2 + hf]
                    S_.tt("dve", xs.t[0:npart, sub, hf * 512:(hf + 1) * 512],
                          xs.t[0:npart, sub, hf * 512:(hf + 1) * 512], bank(k)[0:npart, :], ALU.add,
                          r=[hb, PB[k]], w=[hb])
            rms_stats(lambda sub: xs.t[0:npart, sub, :], nsub, npart,
                      lambda sub: [xs.bufs[sub * 2], xs.bufs[sub * 2 + 1]], 4)
            normalize_transpose(nsub, npart, 4)
            for s in range(8):
                wt = load_wblock(WB_UP + s)
                for j in range(4):
                    fk = s * 4 + j
                    k = gbank()
                    for dc in range(8):
                        S_.mm(bank(k)[:, 0:ntok], wt.t[:, dc, j * 128:(j + 1) * 128], xnT.t[:, dc, 0:ntok],
                              dc == 0, dc == 7, r=[wt.buf, xnT.bufs[dc]], w=[PB[k]])
                    rt = rtmp.next()
                    S_.act(rt.t[:, 0:ntok], bank(k)[:, 0:ntok], AF.Relu, r=[PB[k]], w=[rt.buf])
                    S_.tt("pool", actT.t[:, fk, 0:ntok], rt.t[:, 0:ntok], rt.t[:, 0:ntok], ALU.mult,
                          r=[rt.buf], w=[actT.bufs[fk]])
            for hf in range(2):
                for g in range(4):
                    wt = load_wblock(WB_DN + g * 2 + hf)
                    for sub in range(nsub):
                        for fkk in range(8):
                            fk = g * 8 + fkk
                            S_.mm(bank(sub)[0:npart, :], actT.t[:, fk, sub * npart:(sub + 1) * npart],
                                  wt.t[:, fkk, :], g == 0 and fkk == 0, g == 3 and fkk == 7,
                                  r=[actT.bufs[fk], wt.buf], w=[PB[sub]])
                for sub in range(nsub):
                    hb = xs.bufs[sub * 2 + hf]
                    S_.tt("dve", xs.t[0:npart, sub, hf * 512:(hf + 1) * 512],
                          xs.t[0:npart, sub, hf * 512:(hf + 1) * 512], bank(sub)[0:npart, :], ALU.add,
                          r=[hb, PB[sub]], w=[hb])
            set_gb([5, 6, 7])
            rms_stats(lambda sub: xs.t[0:npart, sub, :], nsub, npart,
                      lambda sub: [xs.bufs[sub * 2], xs.bufs[sub * 2 + 1]], 8)
            for sub in range(nsub):
                S_.stt("dve", xs.t[0:npart, sub, :], xs.t[0:npart, sub, :], stat.t[0:npart, 8 + sub:9 + sub],
                       gfin.t[0:npart, :], ALU.mult, ALU.mult,
                       r=[xs.bufs[sub * 2], xs.bufs[sub * 2 + 1], stat.buf, gfin.buf],
                       w=[xs.bufs[sub * 2], xs.bufs[sub * 2 + 1]])
                S_.dma("pool", y_dst_fn(sub), xs.t[0:npart, sub, :],
                       r=[xs.bufs[sub * 2], xs.bufs[sub * 2 + 1]])

        class _Stop(Exception):
            pass

        def stage(n):
            if KSTAGE <= n:
                raise _Stop()

        try:
          stage(0)
          for b in range(NB):
            for i in range(NBLK):
                t0 = i * 512
                S_.dma("sp", xs.t[:], xp[b, t0:t0 + 512, :].rearrange("(s p) d -> p s d", p=128),
                       w=[xs.buf] + xs.bufs)
                set_gb([5, 6, 7, 0, 1, 2, 3, 4])
                rms_stats(lambda sub: xs.t[:, sub, :], 4, 128,
                          lambda sub: [xs.bufs[sub * 2], xs.bufs[sub * 2 + 1]], 0)
                stage(0.5)
                normalize_transpose(4, 128, 0)
                stage(1)
                for kind, dst in ((4, nkp), (5, nvp)):
                    for hf in range(2):
                        wt = load_wblock(WB_IN + kind * 2 + hf)
                        for sub in range(4):
                            k = gbank()
                            for dc in range(8):
                                S_.mm(bank(k), xnT.t[:, dc, sub * 128:(sub + 1) * 128], wt.t[:, dc, :],
                                      dc == 0, dc == 7, r=[xnT.bufs[dc], wt.buf], w=[PB[k]])
                            st = stg.next()
                            S_.cp("act", st.t[:], bank(k), r=[PB[k]], w=[st.buf])
                            S_.dma("pool", dst[b, t0 + sub * 128:t0 + (sub + 1) * 128, hf * 512:(hf + 1) * 512],
                                   st.t[:], r=[st.buf])
                            if kind == 5 and os.environ.get("K_NOVSC") != "1":
                                vb = vbf.next()
                                S_.cp("dve", vb.t[:], bank(k), r=[PB[k]], w=[vb.buf])
                                kb = i * 4 + sub
                                if os.environ.get("K_NOVDMA") != "1":
                                  S_.dma("pool", vsc[b, t0 + sub * 128:t0 + (sub + 1) * 128, hf * 512:(hf + 1) * 512],
                                       vb.t[:], r=[vb.buf], w=[VSB[b][kb][hf]])
                stage(2)
                def gslot(j):
                    ap = actT.t[:, 2 * j:2 * j + 2, :].rearrange("p a n -> p (a n)").bitcast(F32)
                    return ap, list(actT.bufs[2 * j:2 * j + 2])

                for kind, base in ((6, 0), (7, 8)):
                    for hf in range(2):
                        wt = load_wblock(WB_IN + kind * 2 + hf)
                        for cc in range(4):
                            k = gbank()
                            for dc in range(8):
                                S_.mm(bank(k), wt.t[:, dc, cc * 128:(cc + 1) * 128], xnT.t[:, dc, :],
                                      dc == 0, dc == 7, r=[wt.buf, xnT.bufs[dc]], w=[PB[k]])
                            gap, gbufs = gslot(base + hf * 4 + cc)
                            S_.act(gap, bank(k), AF.Sigmoid, r=[PB[k]], w=gbufs)
                set_gb([5, 6, 7])
                def proj_items(c, banks, b=b, i=i, t0=t0):
                    QT = QTr.next()
                    cvp = cvpr.next()
                    u = ut.next()
                    kt = ktmp.next()
                    W = {}
                    bi = [0]

                    def nb():
                        k = banks[bi[0] % len(banks)]
                        bi[0] += 1
                        return k
                    K = {}

                    def loads():
                        for nm, kind in (("cg", 1), ("hc", 2), ("bg", 0), ("q", 3), ("k", 4)):
                            W[nm] = load_wchunk(kind, c)

                    def mmh(nm, half):
                        def f():
                            if half == 0:
                                K[nm] = nb()
                            k = K[nm]
                            wt = W[nm]
                            for dc in range(4 * half, 4 * half + 4):
                                S_.mm(bank(k), wt.t[:, dc, :], xnT.t[:, dc, :], dc == 0, dc == 7,
                                      r=[wt.buf, xnT.bufs[dc]], w=[PB[k]])
                        return f

                    def e_cg():
                        k = K["cg"]
                        S_.cp("dve", cgs.t[:], bank(k), r=[PB[k]], w=[cgs.buf])

                    def e_hc():
                        k = K["hc"]
                        if i == 0:
                            S_.memset("pool", u.t[:, 0:2], 0.0, w=[u.buf])
                        else:
                            S_.cp("pool", u.t[:, 0:2], carry.t[:, c, :], r=[carry.bufs[c]], w=[u.buf])
                        S_.tt("dve", u.t[:, 2:514], bank(k), cgs.t[:], ALU.mult, r=[PB[k], cgs.buf], w=[u.buf])
                        S_.cp("pool", carry.t[:, c, :], u.t[:, 512:514], r=[u.buf], w=[carry.bufs[c]])

                    def taps12():
                        S_.ts("dve", tA.t[:], u.t[:, 0:512], cw.t[:, c, 0:1], ALU.mult, r=[u.buf, cw.buf], w=[tA.buf])
                        S_.stt("dve", tA.t[:], u.t[:, 1:513], cw.t[:, c, 1:2], tA.t[:], ALU.mult, ALU.add,
                               r=[u.buf, cw.buf, tA.buf], w=[tA.buf])

                    def tap3():
                        S_.stt("dve", tA.t[:], u.t[:, 2:514], cw.t[:, c, 2:3], tA.t[:], ALU.mult, ALU.add,
                               r=[u.buf, cw.buf, tA.buf], w=[tA.buf])

                    def e_bg():
                        k = K["bg"]
                        S_.tt("dve", tB.t[:], bank(k), tA.t[:], ALU.mult, r=[PB[k], tA.buf], w=[tB.buf])
                        gc_ap, gc_bufs = gslot(c)
                        S_.tt("pool", cvp.t[:], tB.t[:], gc_ap, ALU.mult, r=[tB.buf] + gc_bufs, w=[cvp.buf])

                    def e_q():
                        k = K["q"]
                        S_.ts("dve", QT.t[:], bank(k), HD ** -0.5, ALU.mult, r=[PB[k]], w=[QT.buf])

                    def e_k():
                        k = K["k"]
                        S_.cp("dve", kt.t[:], bank(k), r=[PB[k]], w=[kt.buf])
                        S_.dma("pool", kts[b, c, :, t0:t0 + 512], kt.t[:], r=[kt.buf], w=[KTS[b][c][i]])

                    def both(f, g):
                        def h():
                            f()
                            g()
                        return h
                    items = [loads, mmh("cg", 0), mmh("cg", 1), e_cg,
                             mmh("hc", 0), mmh("hc", 1), e_hc,
                             both(mmh("bg", 0), taps12), both(mmh("bg", 1), tap3), e_bg,
                             mmh("q", 0), mmh("q", 1), e_q,
                             mmh("k", 0), mmh("k", 1), e_k]
                    return items, (c, QT, gslot(8 + c), cvp)

                items0, cur0 = proj_items(0, [5, 6, 7])
                for it in items0:
                    it()
                pend = cur0
                for c in range(1, 8 + 1):
                    if c < 8:
                        work, cur = proj_items(c, [7])
                    else:
                        work, cur = [], None
                    work = list(work)
                    if pend is not None and KSTAGE > 3:
                        pc, pQT, psga, pcvp = pend
                        OB = 4
                        for hh in range(2):
                            S_.mm(bank(OB)[64 * hh:64 * hh + 64, :], zer_b.t[:, 0:64], pQT.t[:, :], True, False,
                                  r=[zer_b.buf, pQT.buf], w=[PB[OB]])
                        S_.memset("pool", R32.t[:], 0.0, w=[R32.buf])
                        tiles = [(g, m) for g in range(i, -1, -1) for m in (3, 2, 1, 0)]
                        T = len(tiles)
                        st = [None] * T
                        pieces = {}
                        ZR = (0, 2, 5)

                        def stageA(t, b=b, pc=pc, pQT=pQT, i=i, tiles=tiles, st=st, pieces=pieces):
                            g, m = tiles[t]
                            if g not in pieces:
                                KTp = ktp.next()
                                S_.dma("sp", KTp.t[:], kts[b, pc, :, g * 512:(g + 1) * 512], r=[KTS[b][pc][g]],
                                       w=[KTp.buf])
                                Vp = vtp.next()
                                S_.dma("sp", Vp.t[:], vsc[b, g * 512:(g + 1) * 512, pc * 128:(pc + 1) * 128]
                                       .rearrange("(m p) n -> p m n", p=128),
                                       r=[VSB[b][4 * g + mm][pc // 4] for mm in range(4)], w=[Vp.buf])
                                pieces[g] = (KTp, Vp)
                            KTp, Vp = pieces[g]
                            kb = 4 * g + m
                            diag = (g == i)
                            c0 = 128 * m if diag else 0
                            zk = ZR[t % 3]
                            zb = ps[:, zk * 512:(zk + 2) * 512].rearrange("p (h n) -> p h n", h=2)
                            zB = [PB[zk], PB[zk + 1]]
                            for hh in range(2):
                                S_.mm(zb[:, hh, c0:512], KTp.t[64 * hh:64 * hh + 64, m * 128:(m + 1) * 128],
                                      pQT.t[64 * hh:64 * hh + 64, c0:512], True, not diag,
                                      r=[KTp.buf, pQT.buf], w=[zB[hh]])
                                if diag:
                                    S_.mm(zb[:, hh, c0:c0 + 128], ident_b, m0_b, False, True,
                                          r=[cbf.buf], w=[zB[hh]])
                            e = er.next()
                            S_.act(e.t[:, :, c0:512], zb[:, :, c0:512], AF.Exp, r=zB, w=[e.buf])
                            sp_ = spr.next()
                            S_.act(sp_.t[:, :, c0:512], e.t[:, :, c0:512], AF.Ln, r=[e.buf],
                                   w=[sp_.buf], bias=1.0)
                            Rb = None
                            if kb > 0:
                                S_.tt("dve", R32.t[:, :, c0:512], R32.t[:, :, c0:512], sp_.t[:, :, c0:512],
                                      ALU.add, r=[R32.buf, sp_.buf], w=[R32.buf])
                                Rb = rbr.next()
                                S_.cp("dve", Rb.t[:], R32.t[:], r=[R32.buf], w=[Rb.buf])
                            st[t] = dict(zb=zb, zB=zB, c0=c0, sp=sp_, Rb=Rb, Vp=Vp, m=m, kb=kb)

                        def stageB(t, st=st):
                            d = st[t]
                            zb, zB, c0, sp_ = d["zb"], d["zB"], d["c0"], d["sp"]
                            first = (t == 0)
                            for hh in range(2):
                                S_.mm(zb[:, hh, c0:512], ntri_b, sp_.t[:, hh, c0:512], False, first,
                                      r=[cbf.buf, sp_.buf], w=[zB[hh]])
                                if not first:
                                    Rb = st[t - 1]["Rb"]
                                    S_.mm(zb[:, hh, c0:512], nones_b, Rb.t[:, hh, c0:512], False, True,
                                          r=[cbf.buf, Rb.buf], w=[zB[hh]])
                            a = ar.next()
                            S_.act(a.t[:, :, c0:512], zb[:, :, c0:512], AF.Exp, r=zB, w=[a.buf])
                            d["a"] = a

                        def stageC(t, st=st):
                            d = st[t]
                            a, c0, Vp, m = d["a"], d["c0"], d["Vp"], d["m"]
                            for hh in range(2):
                                S_.mm(bank(OB)[64 * hh:64 * hh + 64, c0:512], Vp.t[:, m, 64 * hh:64 * hh + 64],
                                      a.t[:, hh, c0:512], False, d["kb"] == 0,
                                      r=[Vp.buf, a.buf], w=[PB[OB]])

                        for t in range(T + 2):
                            if t < T:
                                stageA(t)
                            if 0 <= t - 1 < T:
                                stageB(t - 1)
                            if 0 <= t - 2 < T:
                                stageC(t - 2)
                            if work:
                                work.pop(0)()
                        while work:
                            work.pop(0)()
                        S_.tt("dve", mtmp.t[:], bank(OB), psga[0], ALU.mult, r=[PB[OB]] + psga[1], w=[mtmp.buf])
                        S_.tt("pool", mixT.t[:, pc, :], mtmp.t[:], pcvp.t[:], ALU.add,
                              r=[mtmp.buf, pcvp.buf], w=[mixT.bufs[pc]])
                    pend = cur
                stage(4)
                if i == NBLK - 1:
                    for c in range(8):
                        S_.dma("sp", ncp[2 * b:2 * b + 2, c * 128:(c + 1) * 128].rearrange("q p -> p q"),
                               carry.t[:, c, :], r=[carry.bufs[c]], allow_slow_non_contiguous=True)
                stage(5)
                ffn_and_out(4, 128, lambda sub, b=b, t0=t0: yp[b, t0 + sub * 128:t0 + (sub + 1) * 128, :])


          stage(6)

          def flat(t):
              return t.t[:].rearrange("p a n -> p (a n)")

          S_.dma("sp", xs.t[0:NS, 0, :], xsm[:, :], w=[xs.buf] + xs.bufs)
          rms_stats(lambda sub: xs.t[0:NS, 0, :], 1, NS, lambda sub: [xs.bufs[0], xs.bufs[1]], 0)
          normalize_transpose(1, NS, 0)
          S_.dma("sp", cct.t[:], cconv[:, :], w=[cct.buf])
          for c in range(8):
              k = gbank()
              S_.tr(bank(k)[:, 0:SB * 2], cct.t[0:SB * 2, c * 128:(c + 1) * 128], ident_f[0:SB * 2, 0:SB * 2],
                    r=[cct.buf, cst32.buf], w=[PB[k]])
              S_.cp("dve", ccT.t[:, c, :], bank(k)[:, 0:SB * 2], r=[PB[k]], w=[ccT.buf])
          stage(6.1)
          for kind, dst in ((4, nks), (5, nvs)):
              for hf in range(2):
                  wt = load_wblock(WB_IN + kind * 2 + hf)
                  for s in range(SB):
                      k = gbank()
                      for dc in range(8):
                          S_.mm(bank(k)[0:16, :], xnT.t[:, dc, 16 * s:16 * s + 16], wt.t[:, dc, :], dc == 0, dc == 7,
                                r=[xnT.bufs[dc], wt.buf], w=[PB[k]])
                      st = stg.next()
                      S_.cp("act", st.t[0:16, :], bank(k)[0:16, :], r=[PB[k]], w=[st.buf])
                      S_.dma("pool", dst[16 * s:16 * s + 16, hf * 512:(hf + 1) * 512], st.t[0:16, :], r=[st.buf])
                      if kind == 5:
                          S_.cp("dve", vnew.t[0:16, s, hf * 512:(hf + 1) * 512], bank(k)[0:16, :],
                                r=[PB[k]], w=[vnew.bufs[s]])
          stage(6.2)
          for c in range(8):
              w_cg = load_wchunk(1, c)
              w_hc = load_wchunk(2, c)
              w_gc = load_wchunk(6, c)
              w_bg = load_wchunk(0, c)
              w_ga = load_wchunk(7, c)
              w_q = load_wchunk(3, c)
              w_k = load_wchunk(4, c)
              k = proj_fm(w_cg, NS)
              S_.cp("act", cgs.t[:, 0:NS], bank(k)[:, 0:NS], r=[PB[k]], w=[cgs.buf])
              S_.cp("pool", us.t[:, :, 0:2], ccT.t[:, c, :].rearrange("p (s r) -> p s r", r=2),
                    r=[ccT.buf], w=[us.buf])
              k = proj_fm(w_hc, NS)
              S_.tt("dve", us.t[:, :, 2:18], bank(k)[:, 0:NS].rearrange("p (s t) -> p s t", t=16),
                    cgs.t[:, 0:NS].rearrange("p (s t) -> p s t", t=16), ALU.mult, r=[PB[k], cgs.buf], w=[us.buf])
              S_.cp("pool", carrs.t[:, c, :].rearrange("p (s r) -> p s r", r=2), us.t[:, :, 16:18],
                    r=[us.buf], w=[carrs.buf])
              tAv = tA.t[:, 0:NS].rearrange("p (s t) -> p s t", t=16)
              S_.ts("dve", tAv, us.t[:, :, 0:16], cw.t[:, c, 0:1], ALU.mult, r=[us.buf, cw.buf], w=[tA.buf])
              S_.stt("dve", tAv, us.t[:, :, 1:17], cw.t[:, c, 1:2], tAv, ALU.mult, ALU.add,
                     r=[us.buf, cw.buf, tA.buf], w=[tA.buf])
              S_.stt("dve", tAv, us.t[:, :, 2:18], cw.t[:, c, 2:3], tAv, ALU.mult, ALU.add,
                     r=[us.buf, cw.buf, tA.buf], w=[tA.buf])
              k = proj_fm(w_gc, NS)
              S_.act(sgc.t[:, 0:NS], bank(k)[:, 0:NS], AF.Sigmoid, r=[PB[k]], w=[sgc.buf])
              k = proj_fm(w_bg, NS)
              S_.tt("dve", tB.t[:, 0:NS], bank(k)[:, 0:NS], tA.t[:, 0:NS], ALU.mult, r=[PB[k], tA.buf], w=[tB.buf])
              S_.tt("pool", cvps.t[:, c, :], tB.t[:, 0:NS], sgc.t[:, 0:NS], ALU.mult,
                    r=[tB.buf, sgc.buf], w=[cvps.buf])
              k = proj_fm(w_ga, NS)
              S_.act(sgas.t[:, c, :], bank(k)[:, 0:NS], AF.Sigmoid, r=[PB[k]], w=[sgas.buf])
              k = proj_fm(w_q, NS)
              S_.ts("dve", QTs.t[:, c, :], bank(k)[:, 0:NS], HD ** -0.5, ALU.mult, r=[PB[k]], w=[QTs.buf])
              k = proj_fm(w_k, NS)
              S_.cp("act", KTn.t[:, c, :], bank(k)[:, 0:NS], r=[PB[k]], w=[KTn.buf])
          stage(6.3)
          for c in range(8):
              S_.dma("sp", ncs[:, c * 128:(c + 1) * 128].rearrange("q p -> p q"), carrs.t[:, c, :],
                     r=[carrs.buf], allow_slow_non_contiguous=True)
          stage(6.4)
          zi = 0
          OB = 4
          for s in range(SB):
              for hh in range(2):
                  S_.mm(bank(OB)[64 * hh:64 * hh + 64, 0:128], zer2.t[:, 0:64], zer2.t[:, 0:128], True, False,
                        r=[zer2.buf], w=[PB[OB]])
              S_.memset("pool", tB.t[:, 0:256], 0.0, w=[tB.buf])
              first = True
              Rb = None
              for kb in range(NKB_S, -1, -1):
                  new = (kb == NKB_S)
                  nk = 16 if new else 128
                  if new:
                      def kt_ap(c, hh, s=s):
                          return KTn.t[64 * hh:64 * hh + 64, c, 16 * s:16 * s + 16]

                      def v_ap(h, s=s):
                          return vnew.t[0:16, s, 64 * h:64 * h + 64]
                      rk = [KTn.buf]
                      rv = [vnew.bufs[s]]
                  else:
                      stK = er.next()
                      S_.dma("sp", flat(stK), ck[s, kb * 128:(kb + 1) * 128, :], w=[stK.buf])
                      kbf = spr.next()
                      S_.cp("dve", flat(kbf), flat(stK), r=[stK.buf], w=[kbf.buf])
                      ktl = rbr.next()
                      for hf2 in range(2):
                          k = gbank()
                          pT = bank(k).bitcast(BF16)
                          for c4 in range(4):
                              c = hf2 * 4 + c4
                              S_.tr(pT[:, c4 * 128:(c4 + 1) * 128], flat(kbf)[:, c * 128:(c + 1) * 128], ident_b,
                                    r=[kbf.buf, cbf.buf], w=[PB[k]])
                          S_.cp("dve", flat(ktl)[:, hf2 * 512:(hf2 + 1) * 512], pT[:, 0:512], r=[PB[k]], w=[ktl.buf])
                      stV = er.next()
                      S_.dma("sp", flat(stV), cv[s, kb * 128:(kb + 1) * 128, :], w=[stV.buf])
                      vtl = ar.next()
                      S_.cp("dve", flat(vtl), flat(stV), r=[stV.buf], w=[vtl.buf])
                      ktv = flat(ktl).rearrange("p (c n) -> p c n", c=8)
                      vfl = flat(vtl)

                      def kt_ap(c, hh, ktv=ktv):
                          return ktv[64 * hh:64 * hh + 64, c, :]

                      def v_ap(h, vfl=vfl):
                          return vfl[:, 64 * h:64 * h + 64]
                      rk = [ktl.buf]
                      rv = [vtl.buf]
                  zk = 2 * zi
                  zi = 1 - zi
                  zb = ps[:, zk * 512:(zk + 2) * 512].rearrange("p (h n) -> p h n", h=2)
                  zB = [PB[zk], PB[zk + 1]]
                  for hh in range(2):
                      S_.mm(zb[0:nk, hh, 0:128], zer2.t[:, 0:nk], zer2.t[:, 0:128], True, False,
                            r=[zer2.buf], w=[zB[hh]])
                  for c in range(8):
                      for hh in range(2):
                          S_.mm(zb[0:nk, hh, c * 16:(c + 1) * 16], kt_ap(c, hh),
                                QTs.t[64 * hh:64 * hh + 64, c, 16 * s:16 * s + 16], False, False,
                                r=rk + [QTs.buf], w=[zB[hh]])
                          if new:
                              S_.mm(zb[0:16, hh, c * 16:(c + 1) * 16], ident_b[64 * hh:64 * hh + 16, 64 * hh:64 * hh + 16],
                                    m0_b[64 * hh:64 * hh + 16, 64 * hh:64 * hh + 16],
                                    False, False, r=[cbf.buf], w=[zB[hh]])
                  ev = tA.t[:, 0:256].rearrange("p (h n) -> p h n", h=2)
                  S_.act(ev[0:nk], zb[0:nk, :, 0:128], AF.Exp, r=zB, w=[tA.buf])
                  spt = rtmp.next()
                  spv = spt.t[:, 0:256].rearrange("p (h n) -> p h n", h=2)
                  S_.act(spv[0:nk], ev[0:nk], AF.Ln, r=[tA.buf], w=[spt.buf], bias=1.0)
                  for hh in range(2):
                      S_.mm(zb[0:nk, hh, 0:128], ntri_b[0:nk, 0:nk], spv[0:nk, hh, :], False, first,
                            r=[cbf.buf, spt.buf], w=[zB[hh]])
                      if not first:
                          S_.mm(zb[0:nk, hh, 0:128], nones_b[:, 0:nk], Rb.t[:, hh * 128:(hh + 1) * 128], False, True,
                                r=[cbf.buf, Rb.buf], w=[zB[hh]])
                  at = vbf.next()
                  av = at.t[:, 0:256].rearrange("p (h n) -> p h n", h=2)
                  S_.act(av[0:nk], zb[0:nk, :, 0:128], AF.Exp, r=zB, w=[at.buf])
                  if kb > 0:
                      S_.tt("pool", tB.t[0:nk, 0:256], tB.t[0:nk, 0:256], spt.t[0:nk, 0:256], ALU.add,
                            r=[tB.buf, spt.buf], w=[tB.buf])
                      Rb = ktmp.next()
                      S_.cp("dve", Rb.t[:, 0:256], tB.t[:, 0:256], r=[tB.buf], w=[Rb.buf])
                  for c in range(8):
                      for hh in range(2):
                          h = 2 * c + hh
                          S_.mm(bank(OB)[64 * hh:64 * hh + 64, c * 16:(c + 1) * 16], v_ap(h),
                                av[0:nk, hh, c * 16:(c + 1) * 16], False, kb == 0, r=rv + [at.buf], w=[PB[OB]])
                  first = False
              ov = bank(OB)[:, 0:128].rearrange("p (c t) -> p c t", t=16)
              mt = mtmp.t[:, 0:128].rearrange("p (c t) -> p c t", t=16)
              S_.tt("dve", mt, ov, sgas.t[:, :, 16 * s:16 * s + 16], ALU.mult, r=[PB[OB], sgas.buf], w=[mtmp.buf])
              S_.tt("pool", mixT.t[:, :, 16 * s:16 * s + 16], mt, cvps.t[:, :, 16 * s:16 * s + 16], ALU.add,
                    r=[mtmp.buf, cvps.buf], w=list(mixT.bufs))
          stage(6.5)
          ffn_and_out(1, NS, lambda sub: ys[0:NS, :])
        except _Stop:
            pass
        S_.emit()
    return nc


def _consts():
    c = np.zeros((128, 640), np.float32)
    j = np.arange(128)[:, None]
    s = np.arange(128)[None, :]
    c[:, 0:128] = np.eye(128, dtype=np.float32)
    c[:, 128:256] = np.where(j >= s, -1.0, 0.0)
    c[:, 256:384] = -1.0
    c[:, 384:512] = np.where(j >= s, NEG, 0.0)
    c[:, 512:640] = np.eye(128, dtype=np.float32)
    return c


_NC_CACHE = {}


def run_cores(x_prompt, x_sample, cache_conv, cache_k, cache_v, g_mix, w_in, conv_w, w_out,
              g_ffn, w_up, w_down, g_final, n_cores):
    f = lambda a: np.ascontiguousarray(np.asarray(a, dtype=np.float32))
    x_prompt, x_sample, cache_conv, cache_k, cache_v = map(f, (x_prompt, x_sample, cache_conv, cache_k, cache_v))
    B, S, _ = x_prompt.shape
    SBT = x_sample.shape[0]
    PAST = cache_k.shape[2]
    NB = B // n_cores
    SB = SBT // n_cores
    key = (NB, S, SB, PAST)
    if key not in _NC_CACHE:
        _NC_CACHE[key] = build_nc(*key)
    nc = _NC_CACHE[key]
    shared = {
        "g_mix": f(g_mix[0]), "w_in": f(w_in[0]), "conv_w": f(conv_w[0]), "w_out": f(w_out[0]),
        "g_ffn": f(g_ffn[0]), "w_up": f(w_up[0]), "w_down": f(w_down[0]), "g_final": f(g_final),
        "consts": _consts(),
    }
    in_maps = []
    for c in range(n_cores):
        m = dict(shared)
        m["xp"] = x_prompt[c * NB:(c + 1) * NB]
        m["xsm"] = x_sample[c * SB:(c + 1) * SB].reshape(SB * DEC, D)
        m["cconv"] = cache_conv[0, c * SB:(c + 1) * SB].reshape(SB * 2, D)
        m["ck"] = cache_k[0, c * SB:(c + 1) * SB].reshape(SB, PAST, D)
        m["cv"] = cache_v[0, c * SB:(c + 1) * SB].reshape(SB, PAST, D)
        in_maps.append(m)
    res = run_bass_kernel_spmd(nc, in_maps, core_ids=list(range(n_cores)))
    R = res.results
    cat = lambda name: np.concatenate([np.asarray(r[name]) for r in R], axis=0)
    y_prompt = cat("yp").reshape(B, S, D)
    y_sample = cat("ys").reshape(SBT, DEC, D)
    ncp = cat("ncp").reshape(1, B, 2, D)
    nkp = cat("nkp").reshape(1, B, S, NH, HD)
    nvp = cat("nvp").reshape(1, B, S, NH, HD)
    ncs = cat("ncs").reshape(1, SBT, 2, D)
    nks = cat("nks").reshape(1, SBT, DEC, NH, HD)
    nvs = cat("nvs").reshape(1, SBT, DEC, NH, HD)
    return tuple(np.ascontiguousarray(a, dtype=np.float32)
                 for a in (y_prompt, y_sample, ncp, nkp, nvp, ncs, nks, nvs))


def kernel(x_prompt, x_sample, cache_conv, cache_k, cache_v, g_mix, w_in, conv_w, w_out,
           g_ffn, w_up, w_down, g_final):
    return run_cores(x_prompt, x_sample, cache_conv, cache_k, cache_v, g_mix, w_in, conv_w, w_out,
                     g_ffn, w_up, w_down, g_final, N_CORES)
```

```python
import contextlib
import numpy as np
import concourse.bass as bass
import concourse.mybir as mybir
from concourse.bass_utils import run_bass_kernel_spmd

F32 = mybir.dt.float32
BF16 = mybir.dt.bfloat16
AF = mybir.ActivationFunctionType
ALU = mybir.AluOpType
AX = mybir.AxisListType

D = 1024
NH = 16
HD = 64
DFF = 4096
NPROJ = 8192
DEC = 16
EPS = 1e-6
NEG = -30000.0
N_CORES = 8
import os
KSTAGE = float(os.environ.get('KSTAGE', '99'))
KATT = int(os.environ.get('KATT', '99'))
KSKIP = set(os.environ.get('KSKIP', '').split(','))
EPOCH = 30000


class Buf:
    __slots__ = ("name", "w", "rs", "excl")

    def __init__(self, name="", excl=False):
        self.name = name
        self.w = None
        self.rs = []
        self.excl = excl


class Op:
    __slots__ = ("eng", "fn", "deps", "dma", "sem", "val", "signal", "k")

    def __init__(self, eng, fn, dma):
        self.eng = eng
        self.fn = fn
        self.dma = dma
        self.deps = ()
        self.sem = None
        self.val = 0
        self.signal = False
        self.k = -1


class Sched:
    ENGS = ("pe", "act", "dve", "pool", "sp")

    def __init__(self, nc, es, n_dma_sems=10, n_epochs=5):
        self.nc = nc
        self.ops = {e: [] for e in self.ENGS}
        self.pending = {e: set() for e in self.ENGS}
        self.last = {e: None for e in self.ENGS}
        self.esems = {e: [es.enter_context(nc.semaphore(f"s_{e}{j}")) for j in range(n_epochs)]
                      for e in ("pe", "act", "dve", "pool")}
        self.dsems = {q: [es.enter_context(nc.semaphore(f"d_{q}{j}")) for j in range(n_dma_sems)]
                      for q in ("sp", "pool")}
        self.dcnt = {q: [0] * n_dma_sems for q in ("sp", "pool")}
        self.dlast = {q: [None] * n_dma_sems for q in ("sp", "pool")}
        self.drr = {q: 0 for q in ("sp", "pool")}
        self.all_dma = []

    def _deps(self, eng, r, w, is_dma):
        raw = set()
        other = set()
        for b in r:
            if b.w is not None:
                raw.add(b.w)
            if b.excl:
                other.update(b.rs)
        for b in w:
            if b.w is not None:
                other.add(b.w)
            other.update(b.rs)
        deps = set()
        for d in raw | other:
            if d.dma:
                deps.add(d)
            elif d.eng != eng:
                deps.add(d)
            else:
                if is_dma or eng != "pe":
                    deps.add(d)
        return deps

    def _record(self, o, r, w):
        deps = self._deps(o.eng, r, w, o.dma)
        if self.pending[o.eng]:
            deps |= self.pending[o.eng]
            self.pending[o.eng] = set()
        o.deps = tuple(deps)
        for d in deps:
            d.signal = True
        self.ops[o.eng].append(o)
        self.last[o.eng] = o
        for b in w:
            b.w = o
            b.rs = []
        for b in r:
            if b.w is not o:
                b.rs.append(o)
        return o

    def op(self, eng, fn, r=(), w=()):
        return self._record(Op(eng, fn, False), r, w)

    def dma(self, q, out, in_, r=(), w=(), **kw):
        if os.environ.get("KQ_SP", "0") == "1":
            q = "sp"
        o = Op(q, (lambda e, out=out, in_=in_, kw=kw: e.dma_start(out=out, in_=in_, **kw)), True)
        j = self.drr[q]
        self.drr[q] = (j + 1) % len(self.dsems[q])
        prev = self.dlast[q][j]
        self.dcnt[q][j] += 1
        o.sem = self.dsems[q][j]
        o.val = 16 * self.dcnt[q][j]
        self._record(o, r, w)
        if prev is not None:
            o.deps = o.deps + (prev,)
        self.dlast[q][j] = o
        self.all_dma.append(o)
        return o

    def barrier(self):
        lasts = [self.last[e] for e in ("pe", "act", "dve", "pool") if self.last[e] is not None]
        for d in lasts:
            d.signal = True
        dm = [x for q in self.dlast for x in self.dlast[q] if x is not None]
        for e in self.ENGS:
            self.pending[e] |= set(lasts) | set(dm)

    def mm(self, out, lhsT, rhs, start, stop, r, w):
        return self.op("pe", lambda e: e.matmul(out, lhsT=lhsT, rhs=rhs, start=start, stop=stop,
                                                skip_group_check=True), r, w)

    def tr(self, out, in_, ident, r, w):
        return self.op("pe", lambda e: e.transpose(out, in_, ident), r, w)

    def act(self, out, in_, func, r, w, bias=None, scale=None, accum=None):
        kw = {}
        if bias is not None:
            kw["bias"] = bias
        if scale is not None:
            kw["scale"] = scale
        if accum is not None:
            kw["accum_out"] = accum
        return self.op("act", lambda e: e.activation(out=out, in_=in_, func=func, **kw), r, w)

    def tt(self, eng, out, in0, in1, op, r, w):
        return self.op(eng, lambda e: e.tensor_tensor(out=out, in0=in0, in1=in1, op=op), r, w)

    def ts(self, eng, out, in0, s1, op0, r, w, s2=None, op1=None):
        if op1 is None:
            return self.op(eng, lambda e: e.tensor_scalar(out=out, in0=in0, scalar1=s1, scalar2=None,
                                                          op0=op0), r, w)
        return self.op(eng, lambda e: e.tensor_scalar(out=out, in0=in0, scalar1=s1, scalar2=s2,
                                                      op0=op0, op1=op1), r, w)

    def stt(self, eng, out, in0, scalar, in1, op0, op1, r, w):
        return self.op(eng, lambda e: e.scalar_tensor_tensor(out=out, in0=in0, scalar=scalar, in1=in1,
                                                             op0=op0, op1=op1), r, w)

    def cp(self, eng, out, in_, r, w):
        if eng == "act":
            return self.op("act", lambda e: e.activation(out=out, in_=in_, func=AF.Identity), r, w)
        return self.op(eng, lambda e: e.tensor_copy(out=out, in_=in_), r, w)

    def memset(self, eng, ap, val, w):
        return self.op(eng, lambda e: e.memset(ap, val), (), w)

    def recip(self, out, in_, r, w):
        return self.op("dve", lambda e: e.reciprocal(out=out, in_=in_), r, w)

    def emit(self):
        nc = self.nc
        for e in ("pe", "act", "dve", "pool"):
            k = 0
            for o in self.ops[e]:
                if not o.dma and o.signal:
                    o.k = k
                    o.sem = self.esems[e][k // EPOCH]
                    o.val = (k % EPOCH) + 1
                    k += 1
            assert k <= EPOCH * len(self.esems[e]), (e, k)

        def run(ename, eng):
            waited = {}
            for o in self.ops[ename]:
                need = {}
                for d in o.deps:
                    key = id(d.sem)
                    if key not in need or need[key][1] < d.val:
                        need[key] = (d.sem, d.val)
                for key, (sem, val) in need.items():
                    if waited.get(key, 0) >= val:
                        continue
                    eng.wait_ge(sem, val)
                    waited[key] = val
                ins = o.fn(eng)
                if o.dma:
                    ins.then_inc(o.sem, 16)
                elif o.signal:
                    ins.then_inc(o.sem, 1)
            if ename in self.dsems:
                for j, sem in enumerate(self.dsems[ename]):
                    if self.dcnt[ename][j] > 0:
                        eng.wait_ge(sem, 16 * self.dcnt[ename][j])

        with nc.Block() as block:
            @block.tensor
            def _(eng):
                run("pe", eng)

            @block.scalar
            def _(eng):
                run("act", eng)

            @block.vector
            def _(eng):
                run("dve", eng)

            @block.gpsimd
            def _(eng):
                run("pool", eng)

            @block.sync
            def _(eng):
                run("sp", eng)


class Tile:
    __slots__ = ("t", "buf", "bufs")

    def __init__(self, t, name, nb=0):
        self.t = t
        self.buf = Buf(name)
        self.bufs = [Buf(f"{name}{i}") for i in range(nb)]


class Ring:
    def __init__(self, tiles):
        self.tiles = tiles
        self.i = 0

    def next(self):
        t = self.tiles[self.i]
        self.i = (self.i + 1) % len(self.tiles)
        return t


def build_nc(NB, S, SB, PAST):
    nc = bass.Bass("TRN2", target_bir_lowering=False)
    NBLK = S // 512
    NS = SB * DEC
    NKB_S = PAST // 128

    def din(name, shape, dt=F32):
        return nc.dram_tensor(name, list(shape), dt, kind="ExternalInput").ap()

    def dout(name, shape, dt=F32):
        return nc.dram_tensor(name, list(shape), dt, kind="ExternalOutput").ap()

    xp = din("xp", [NB, S, D])
    xsm = din("xsm", [NS, D])
    cconv = din("cconv", [SB * 2, D])
    ck = din("ck", [SB, PAST, D])
    cv = din("cv", [SB, PAST, D])
    g_mix = din("g_mix", [D])
    w_in = din("w_in", [D, NPROJ])
    conv_w = din("conv_w", [3, D])
    w_out = din("w_out", [D, D])
    g_ffn = din("g_ffn", [D])
    w_up = din("w_up", [D, DFF])
    w_down = din("w_down", [DFF, D])
    g_final = din("g_final", [D])
    consts = din("consts", [128, 640])

    yp = dout("yp", [NB, S, D])
    ys = dout("ys", [NS, D])
    ncp = dout("ncp", [NB * 2, D])
    nkp = dout("nkp", [NB, S, D])
    nvp = dout("nvp", [NB, S, D])
    ncs = dout("ncs", [SB * 2, D])
    nks = dout("nks", [NS, D])
    nvs = dout("nvs", [NS, D])

    NWB = 34
    wsc = nc.dram_tensor("wsc", [NWB, 128, 4096], BF16).ap()
    kts = nc.dram_tensor("kts", [NB, 8, 128, S], BF16).ap()
    vsc = nc.dram_tensor("vsc", [NB, S, D], BF16).ap()
    WB_IN, WB_OUT, WB_UP, WB_DN = 0, 16, 18, 26

    es = contextlib.ExitStack()
    with es:
        S_ = Sched(nc, es)

        def sb(name, shape, dt, nb=0):
            return Tile(es.enter_context(nc.sbuf_tensor(name, list(shape), dt)), name, nb)

        def ring(name, shape, dt, n):
            return Ring([sb(f"{name}{i}", shape, dt) for i in range(n)])

        ps = es.enter_context(nc.psum_tensor("ps", [128, 4096], F32))
        PB = [Buf(f"bank{k}", excl=True) for k in range(8)]

        def bank(k):
            return ps[:, k * 512:(k + 1) * 512]

        gb_i = [0]
        GB = [[5, 6, 7]]

        def set_gb(lst):
            GB[0] = list(lst)
            gb_i[0] = 0

        def gbank():
            lst = GB[0]
            k = lst[gb_i[0] % len(lst)]
            gb_i[0] = (gb_i[0] + 1) % len(lst)
            return k

        cst32 = sb("cst32", [128, 640], F32)
        ident_f = cst32.t[:, 512:640]
        cbf = sb("cbf", [128, 512], BF16)
        ident_b = cbf.t[:, 0:128]
        ntri_b = cbf.t[:, 128:256]
        nones_b = cbf.t[:, 256:384]
        m0_b = cbf.t[:, 384:512]
        zer_b = sb("zer_b", [128, 64], BF16)
        cw = sb("cw", [128, 8, 3], F32)
        gmx = sb("gmx", [128, 8], F32)
        gff = sb("gff", [128, 8], F32)
        gfin = sb("gfin", [128, D], F32)
        epsb = sb("epsb", [128, 1], F32)
        oneb = sb("oneb", [128, 1], F32)

        xs = sb("xs", [128, 4, D], F32, nb=8)
        xnb = sb("xnb", [128, 4, D], BF16, nb=4)
        xnT = sb("xnT", [128, 8, 512], BF16, nb=8)
        mixT = sb("mixT", [128, 8, 512], BF16, nb=8)
        actT = sb("actT", [128, 32, 512], BF16, nb=32)
        stat = sb("stat", [128, 16], F32)
        wsl = ring("wsl", [128, 8, 512], BF16, int(os.environ.get("KWSL", "3")))
        wck = ring("wck", [128, 8, 128], BF16, 7)
        stg = ring("stg", [128, 512], F32, 2)
        vbf = ring("vbf", [128, 512], BF16, 2)
        ktmp = ring("ktmp", [128, 512], BF16, 2)
        QTr = ring("QT", [128, 512], BF16, 2)
        sgar = ring("sga", [128, 512], F32, 2)
        cvpr = ring("cvp", [128, 512], F32, 2)
        cgs = sb("cgs", [128, 512], F32)
        sgc = sb("sgc", [128, 512], F32)
        ut = ring("ut", [128, 514], F32, 2)
        tA = sb("tA", [128, 512], F32)
        tB = sb("tB", [128, 512], F32)
        carry = sb("carry", [128, 8, 2], F32, nb=8)
        ktp = ring("ktp", [128, 512], BF16, 3)
        vtp = ring("vtp", [128, 4, 128], BF16, 3)
        er = ring("e", [128, 2, 512], F32, 2)
        spr = ring("sp", [128, 2, 512], BF16, 3)
        ar = ring("a", [128, 2, 512], BF16, 2)
        R32 = sb("R32", [128, 2, 512], F32)
        rbr = ring("Rb", [128, 2, 512], BF16, 3)
        mtmp = sb("mtmp", [128, 512], F32)
        rtmp = ring("rtmp", [128, 512], BF16, 2)

        QTs = sb("QTs", [128, 8, NS], BF16)
        KTn = sb("KTn", [128, 8, NS], BF16)
        sgas = sb("sgas", [128, 8, NS], F32)
        cvps = sb("cvps", [128, 8, NS], F32)
        us = sb("us", [128, SB, 18], F32)
        carrs = sb("carrs", [128, 8, SB * 2], F32)
        cct = sb("cct", [SB * 2, D], F32)
        ccT = sb("ccT", [128, 8, SB * 2], F32)
        vnew = sb("vnew", [16, SB, D], BF16, nb=SB)
        zer2 = sb("zer2", [128, 256], BF16)

        if os.environ.get('KDBG'):
            print('SBUF remaining bytes/partition:', nc.sbuf_bytes_remaining)
        KTS = [[[Buf() for _ in range(NBLK)] for _ in range(8)] for _ in range(NB)]
        VSB = [[[Buf() for _ in range(2)] for _ in range(S // 128)] for _ in range(NB)]
        WSC = [Buf() for _ in range(NWB)]

        S_.dma("sp", cst32.t[:], consts[:, :], w=[cst32.buf])
        S_.cp("dve", cbf.t[:], cst32.t[:, 0:512], r=[cst32.buf], w=[cbf.buf])
        S_.memset("pool", zer_b.t[:], 0.0, w=[zer_b.buf])
        S_.memset("pool", zer2.t[:], 0.0, w=[zer2.buf])
        S_.memset("pool", epsb.t[:], EPS, w=[epsb.buf])
        S_.memset("pool", oneb.t[:], 1.0, w=[oneb.buf])
        for tap in range(3):
            S_.dma("sp", cw.t[:, :, tap], conv_w[tap].rearrange("(c p) -> p c", p=128), w=[cw.buf],
                   allow_slow_non_contiguous=True)
        S_.dma("sp", gmx.t[:], g_mix.rearrange("(c p) -> p c", p=128), w=[gmx.buf],
               allow_slow_non_contiguous=True)
        S_.dma("sp", gff.t[:], g_ffn.rearrange("(c p) -> p c", p=128), w=[gff.buf],
               allow_slow_non_contiguous=True)
        S_.dma("sp", gfin.t[:], g_final.partition_broadcast(128), w=[gfin.buf])

        stage_views = [
            (xs.t[:].rearrange("p a d -> p (a d)").rearrange("p (c n) -> p c n", c=8), [xs.buf] + xs.bufs),
            (actT.t[:, 0:16, :].rearrange("p a n -> p (a n)").bitcast(F32).rearrange("p (c n) -> p c n", c=8),
             actT.bufs[0:16]),
            (actT.t[:, 16:32, :].rearrange("p a n -> p (a n)").bitcast(F32).rearrange("p (c n) -> p c n", c=8),
             actT.bufs[16:32]),
        ]
        wblocks = []
        for j in range(16):
            wblocks.append((WB_IN + j, w_in, 0, j * 512, gmx))
        for j in range(2):
            wblocks.append((WB_OUT + j, w_out, 0, j * 512, None))
        for j in range(8):
            wblocks.append((WB_UP + j, w_up, 0, j * 512, gff))
        for g in range(4):
            for hf in range(2):
                wblocks.append((WB_DN + g * 2 + hf, w_down, g * 1024, hf * 512, None))
        for n, (bi, W, r0, n0, gain) in enumerate(wblocks):
            sv, sbufs = stage_views[n % 3]
            S_.dma("sp", sv, W[r0:r0 + 1024, n0:n0 + 512].rearrange("(c p) n -> p c n", p=128), w=sbufs)
            wt = wsl.next()
            eng = "dve" if n % 2 == 0 else "act"
            if gain is None:
                S_.cp(eng, wt.t[:], sv, r=sbufs, w=[wt.buf])
            else:
                for c in range(8):
                    if eng == "dve":
                        S_.ts(eng, wt.t[:, c, :], sv[:, c, :], gain.t[:, c:c + 1], ALU.mult,
                              r=sbufs + [gain.buf], w=[wt.buf])
                    else:
                        S_.act(wt.t[:, c, :], sv[:, c, :], AF.Identity, r=sbufs + [gain.buf], w=[wt.buf],
                               scale=gain.t[:, c:c + 1])
            S_.dma("sp", wsc[bi].rearrange("p (c n) -> p c n", c=8), wt.t[:], r=[wt.buf], w=[WSC[bi]])

        def rms_stats(src_ap_fn, nsub, npart, rbufs, col0):
            S_.memset("pool", stat.t[:, col0:col0 + nsub], 0.0, w=[stat.buf])
            for sub in range(nsub):
                S_.act(xnb.t[0:npart, sub, :], src_ap_fn(sub), AF.Square, r=rbufs(sub) + [stat.buf],
                       w=[xnb.bufs[sub], stat.buf], accum=stat.t[0:npart, col0 + sub:col0 + sub + 1])
            sl = stat.t[0:npart, col0:col0 + nsub]
            S_.act(sl, sl, AF.Sqrt, r=[stat.buf, epsb.buf], w=[stat.buf], bias=epsb.t[0:npart, :], scale=1.0 / D)
            S_.recip(sl, sl, r=[stat.buf], w=[stat.buf])

        def normalize_transpose(nsub, npart, col0):
            ntok = nsub * npart
            for sub in range(nsub):
                S_.ts("dve", xnb.t[0:npart, sub, :], xs.t[0:npart, sub, :],
                      stat.t[0:npart, col0 + sub:col0 + sub + 1], ALU.mult,
                      r=[xs.bufs[sub * 2], xs.bufs[sub * 2 + 1], stat.buf], w=[xnb.bufs[sub]])
            for dc in range(8):
                k = gbank()
                pT = bank(k).bitcast(BF16)
                for sub in range(nsub):
                    S_.tr(pT[:, sub * npart:(sub + 1) * npart], xnb.t[0:npart, sub, dc * 128:(dc + 1) * 128],
                          ident_b[0:npart, 0:npart], r=[xnb.bufs[sub], cbf.buf], w=[PB[k]])
                if os.environ.get("KT_NOCOPY") == "1":
                    continue
                S_.cp("act" if (dc % 2 == 1 and os.environ.get("KT_ACT") == "1") else "dve",
                      xnT.t[:, dc, 0:ntok], pT[:, 0:ntok], r=[PB[k]], w=[xnT.bufs[dc]])

        def load_wblock(bi):
            wt = wsl.next()
            S_.dma("sp", wt.t[:], wsc[bi].rearrange("p (c n) -> p c n", c=8), r=[WSC[bi]], w=[wt.buf])
            return wt

        def load_wchunk(kind, c):
            bi = WB_IN + kind * 2 + c // 4
            wt = wck.next()
            src = wsc[bi].rearrange("p (c n) -> p c n", c=8)[:, :, (c % 4) * 128:(c % 4) * 128 + 128]
            S_.dma("sp", wt.t[:], src, r=[WSC[bi]], w=[wt.buf])
            return wt

        def proj_fm(wt, ntok):
            k = gbank()
            for dc in range(8):
                S_.mm(bank(k)[:, 0:ntok], wt.t[:, dc, :], xnT.t[:, dc, 0:ntok], dc == 0, dc == 7,
                      r=[wt.buf, xnT.bufs[dc]], w=[PB[k]])
            return k

        xnT_all = list(xnT.bufs)

        def ffn_and_out(nsub, npart, y_dst_fn):
            ntok = nsub * npart
            set_gb([4, 5, 6, 7])
            for hf in range(2):
                wt = load_wblock(WB_OUT + hf)
                for sub in range(nsub):
                    k = gbank()
                    for c in range(8):
                        S_.mm(bank(k)[0:npart, :], mixT.t[:, c, sub * npart:(sub + 1) * npart], wt.t[:, c, :],
                              c == 0, c == 7, r=[mixT.bufs[c], wt.buf], w=[PB[k]])
                    hb = xs.bufs[sub * 2 + hf]
                    S_.tt("dve", xs.t[0:npart, sub, hf * 512:(hf + 1) * 512],
                          xs.t[0:npart, sub, hf * 512:(hf + 1) * 512], bank(k)[0:npart, :], ALU.add,
                          r=[hb, PB[k]], w=[hb])
            rms_stats(lambda sub: xs.t[0:npart, sub, :], nsub, npart,
                      lambda sub: [xs.bufs[sub * 2], xs.bufs[sub * 2 + 1]], 4)
            normalize_transpose(nsub, npart, 4)
            for s in range(8):
                wt = load_wblock(WB_UP + s)
                for j in range(4):
                    fk = s * 4 + j
                    k = gbank()
                    for dc in range(8):
                        S_.mm(bank(k)[:, 0:ntok], wt.t[:, dc, j * 128:(j + 1) * 128], xnT.t[:, dc, 0:ntok],
                              dc == 0, dc == 7, r=[wt.buf, xnT.bufs[dc]], w=[PB[k]])
                    rt = rtmp.next()
                    S_.act(rt.t[:, 0:ntok], bank(k)[:, 0:ntok], AF.Relu, r=[PB[k]], w=[rt.buf])
                    S_.tt("pool", actT.t[:, fk, 0:ntok], rt.t[:, 0:ntok], rt.t[:, 0:ntok], ALU.mult,
                          r=[rt.buf], w=[actT.bufs[fk]])
            for hf in range(2):
                for g in range(4):
                    wt = load_wblock(WB_DN + g * 2 + hf)
                    for sub in range(nsub):
                        for fkk in range(8):
                            fk = g * 8 + fkk
                            S_.mm(bank(sub)[0:npart, :], actT.t[:, fk, sub * npart:(sub + 1) * npart],
                                  wt.t[:, fkk, :], g == 0 and fkk == 0, g == 3 and fkk == 7,
                                  r=[actT.bufs[fk], wt.buf], w=[PB[sub]])
                for sub in range(nsub):
                    hb = xs.bufs[sub * 2 + hf]
                    S_.tt("dve", xs.t[0:npart, sub, hf * 512:(hf + 1) * 512],
                          xs.t[0:npart, sub, hf * 512:(hf + 1) * 512], bank(sub)[0:npart, :], ALU.add,
                          r=[hb, PB[sub]], w=[hb])
            set_gb([5, 6, 7])
            rms_stats(lambda sub: xs.t[0:npart, sub, :], nsub, npart,
                      lambda sub: [xs.bufs[sub * 2], xs.bufs[sub * 2 + 1]], 8)
            for sub in range(nsub):
                S_.stt("dve", xs.t[0:npart, sub, :], xs.t[0:npart, sub, :], stat.t[0:npart, 8 + sub:9 + sub],
                       gfin.t[0:npart, :], ALU.mult, ALU.mult,
                       r=[xs.bufs[sub * 2], xs.bufs[sub * 2 + 1], stat.buf, gfin.buf],
                       w=[xs.bufs[sub * 2], xs.bufs[sub * 2 + 1]])
                S_.dma("pool", y_dst_fn(sub), xs.t[0:npart, sub, :],
                       r=[xs.bufs[sub * 2], xs.bufs[sub * 2 + 1]])

        class _Stop(Exception):
            pass

        def stage(n):
            if KSTAGE <= n:
                raise _Stop()

        try:
          stage(0)
          for b in range(NB):
            for i in range(NBLK):
                t0 = i * 512
                S_.dma("sp", xs.t[:], xp[b, t0:t0 + 512, :].rearrange("(s p) d -> p s d", p=128),
                       w=[xs.buf] + xs.bufs)
                set_gb([5, 6, 7, 0, 1, 2, 3, 4])
                rms_stats(lambda sub: xs.t[:, sub, :], 4, 128,
                          lambda sub: [xs.bufs[sub * 2], xs.bufs[sub * 2 + 1]], 0)
                stage(0.5)
                normalize_transpose(4, 128, 0)
                stage(1)
                for kind, dst in ((4, nkp), (5, nvp)):
                    for hf in range(2):
                        wt = load_wblock(WB_IN + kind * 2 + hf)
                        for sub in range(4):
                            k = gbank()
                            for dc in range(8):
                                S_.mm(bank(k), xnT.t[:, dc, sub * 128:(sub + 1) * 128], wt.t[:, dc, :],
                                      dc == 0, dc == 7, r=[xnT.bufs[dc], wt.buf], w=[PB[k]])
                            st = stg.next()
                            S_.cp("act", st.t[:], bank(k), r=[PB[k]], w=[st.buf])
                            S_.dma("pool", dst[b, t0 + sub * 128:t0 + (sub + 1) * 128, hf * 512:(hf + 1) * 512],
                                   st.t[:], r=[st.buf])
                            if kind == 5 and os.environ.get("K_NOVSC") != "1":
                                vb = vbf.next()
                                S_.cp("dve", vb.t[:], bank(k), r=[PB[k]], w=[vb.buf])
                                kb = i * 4 + sub
                                if os.environ.get("K_NOVDMA") != "1":
                                  S_.dma("pool", vsc[b, t0 + sub * 128:t0 + (sub + 1) * 128, hf * 512:(hf + 1) * 512],
                                       vb.t[:], r=[vb.buf], w=[VSB[b][kb][hf]])
                stage(2)
                set_gb([5, 6, 7])
                pend = None
                for c in range(8 + 1):
                    att_on = pend is not None and KSTAGE > 3
                    if att_on:
                        pc, pQT, psga, pcvp = pend
                        OB = 4
                        for hh in range(2):
                            S_.mm(bank(OB)[64 * hh:64 * hh + 64, :], zer_b.t[:, 0:64], pQT.t[:, :], True, False,
                                  r=[zer_b.buf, pQT.buf], w=[PB[OB]])
                        S_.memset("pool", R32.t[:], 0.0, w=[R32.buf])
                        tiles = [(g, m) for g in range(i, -1, -1) for m in (3, 2, 1, 0)]
                        T = len(tiles)
                        st = [None] * T
                        pieces = {}
                        ZR = (0, 2, 5)

                        def stageA(t, b=b, pc=pc, pQT=pQT, i=i, tiles=tiles, st=st, pieces=pieces):
                            g, m = tiles[t]
                            if g not in pieces:
                                KTp = ktp.next()
                                S_.dma("sp", KTp.t[:], kts[b, pc, :, g * 512:(g + 1) * 512], r=[KTS[b][pc][g]],
                                       w=[KTp.buf])
                                Vp = vtp.next()
                                S_.dma("sp", Vp.t[:], vsc[b, g * 512:(g + 1) * 512, pc * 128:(pc + 1) * 128]
                                       .rearrange("(m p) n -> p m n", p=128),
                                       r=[VSB[b][4 * g + mm][pc // 4] for mm in range(4)], w=[Vp.buf])
                                pieces[g] = (KTp, Vp)
                            KTp, Vp = pieces[g]
                            kb = 4 * g + m
                            diag = (g == i)
                            c0 = 128 * m if diag else 0
                            zk = ZR[t % 3]
                            zb = ps[:, zk * 512:(zk + 2) * 512].rearrange("p (h n) -> p h n", h=2)
                            zB = [PB[zk], PB[zk + 1]]
                            for hh in range(2):
                                S_.mm(zb[:, hh, c0:512], KTp.t[64 * hh:64 * hh + 64, m * 128:(m + 1) * 128],
                                      pQT.t[64 * hh:64 * hh + 64, c0:512], True, not diag,
                                      r=[KTp.buf, pQT.buf], w=[zB[hh]])
                                if diag:
                                    S_.mm(zb[:, hh, c0:c0 + 128], ident_b, m0_b, False, True,
                                          r=[cbf.buf], w=[zB[hh]])
                            e = er.next()
                            S_.act(e.t[:, :, c0:512], zb[:, :, c0:512], AF.Exp, r=zB, w=[e.buf])
                            sp_ = spr.next()
                            S_.act(sp_.t[:, :, c0:512], e.t[:, :, c0:512], AF.Ln, r=[e.buf],
                                   w=[sp_.buf], bias=1.0)
                            Rb = None
                            if kb > 0:
                                S_.tt("dve", R32.t[:, :, c0:512], R32.t[:, :, c0:512], sp_.t[:, :, c0:512],
                                      ALU.add, r=[R32.buf, sp_.buf], w=[R32.buf])
                                Rb = rbr.next()
                                S_.cp("dve", Rb.t[:], R32.t[:], r=[R32.buf], w=[Rb.buf])
                            st[t] = dict(zb=zb, zB=zB, c0=c0, sp=sp_, Rb=Rb, Vp=Vp, m=m, kb=kb)

                        def stageB(t, st=st):
                            d = st[t]
                            zb, zB, c0, sp_ = d["zb"], d["zB"], d["c0"], d["sp"]
                            first = (t == 0)
                            for hh in range(2):
                                S_.mm(zb[:, hh, c0:512], ntri_b, sp_.t[:, hh, c0:512], False, first,
                                      r=[cbf.buf, sp_.buf], w=[zB[hh]])
                                if not first:
                                    Rb = st[t - 1]["Rb"]
                                    S_.mm(zb[:, hh, c0:512], nones_b, Rb.t[:, hh, c0:512], False, True,
                                          r=[cbf.buf, Rb.buf], w=[zB[hh]])
                            a = ar.next()
                            S_.act(a.t[:, :, c0:512], zb[:, :, c0:512], AF.Exp, r=zB, w=[a.buf])
                            d["a"] = a

                        def stageC(t, st=st):
                            d = st[t]
                            a, c0, Vp, m = d["a"], d["c0"], d["Vp"], d["m"]
                            for hh in range(2):
                                S_.mm(bank(OB)[64 * hh:64 * hh + 64, c0:512], Vp.t[:, m, 64 * hh:64 * hh + 64],
                                      a.t[:, hh, c0:512], False, d["kb"] == 0,
                                      r=[Vp.buf, a.buf], w=[PB[OB]])

                        stageA(0)
                        stageA(1)
                        stageB(0)
                    if c < 8:
                        w_cg = load_wchunk(1, c)
                        w_hc = load_wchunk(2, c)
                        w_gc = load_wchunk(6, c)
                        w_bg = load_wchunk(0, c)
                        w_ga = load_wchunk(7, c)
                        w_q = load_wchunk(3, c)
                        w_k = load_wchunk(4, c)
                        k = proj_fm(w_cg, 512)
                        S_.cp("act", cgs.t[:], bank(k), r=[PB[k]], w=[cgs.buf])
                        u = ut.next()
                        if i == 0:
                            S_.memset("pool", u.t[:, 0:2], 0.0, w=[u.buf])
                        else:
                            S_.cp("pool", u.t[:, 0:2], carry.t[:, c, :], r=[carry.bufs[c]], w=[u.buf])
                        k = proj_fm(w_hc, 512)
                        S_.tt("dve", u.t[:, 2:514], bank(k), cgs.t[:], ALU.mult, r=[PB[k], cgs.buf], w=[u.buf])
                        S_.cp("pool", carry.t[:, c, :], u.t[:, 512:514], r=[u.buf], w=[carry.bufs[c]])
                        S_.ts("dve", tA.t[:], u.t[:, 0:512], cw.t[:, c, 0:1], ALU.mult, r=[u.buf, cw.buf], w=[tA.buf])
                        S_.stt("dve", tA.t[:], u.t[:, 1:513], cw.t[:, c, 1:2], tA.t[:], ALU.mult, ALU.add,
                               r=[u.buf, cw.buf, tA.buf], w=[tA.buf])
                        S_.stt("dve", tA.t[:], u.t[:, 2:514], cw.t[:, c, 2:3], tA.t[:], ALU.mult, ALU.add,
                               r=[u.buf, cw.buf, tA.buf], w=[tA.buf])
                        k = proj_fm(w_gc, 512)
                        S_.act(sgc.t[:], bank(k), AF.Sigmoid, r=[PB[k]], w=[sgc.buf])
                        k = proj_fm(w_bg, 512)
                        S_.tt("dve", tB.t[:], bank(k), tA.t[:], ALU.mult, r=[PB[k], tA.buf], w=[tB.buf])
                        cvp = cvpr.next()
                        S_.tt("pool", cvp.t[:], tB.t[:], sgc.t[:], ALU.mult, r=[tB.buf, sgc.buf], w=[cvp.buf])
                        k = proj_fm(w_ga, 512)
                        sga = sgar.next()
                        S_.act(sga.t[:], bank(k), AF.Sigmoid, r=[PB[k]], w=[sga.buf])
                        k = proj_fm(w_q, 512)
                        QT = QTr.next()
                        S_.ts("dve", QT.t[:], bank(k), HD ** -0.5, ALU.mult, r=[PB[k]], w=[QT.buf])
                        k = proj_fm(w_k, 512)
                        kt = ktmp.next()
                        S_.cp("act", kt.t[:], bank(k), r=[PB[k]], w=[kt.buf])
                        S_.dma("pool", kts[b, c, :, t0:t0 + 512], kt.t[:], r=[kt.buf], w=[KTS[b][c][i]])
                        cur = (c, QT, sga, cvp)
                    else:
                        cur = None
                    if att_on:
                        for t in range(2, T + 2):
                            if t < T:
                                stageA(t)
                            if 0 <= t - 1 < T:
                                stageB(t - 1)
                            if 0 <= t - 2 < T:
                                stageC(t - 2)
                        S_.tt("dve", mtmp.t[:], bank(OB), psga.t[:], ALU.mult, r=[PB[OB], psga.buf], w=[mtmp.buf])
                        S_.tt("pool", mixT.t[:, pc, :], mtmp.t[:], pcvp.t[:], ALU.add,
                              r=[mtmp.buf, pcvp.buf], w=[mixT.bufs[pc]])
                    pend = cur
                stage(4)
                if i == NBLK - 1:
                    for c in range(8):
                        S_.dma("sp", ncp[2 * b:2 * b + 2, c * 128:(c + 1) * 128].rearrange("q p -> p q"),
                               carry.t[:, c, :], r=[carry.bufs[c]], allow_slow_non_contiguous=True)
                stage(5)
                ffn_and_out(4, 128, lambda sub, b=b, t0=t0: yp[b, t0 + sub * 128:t0 + (sub + 1) * 128, :])


          stage(6)

          def flat(t):
              return t.t[:].rearrange("p a n -> p (a n)")

          S_.dma("sp", xs.t[0:NS, 0, :], xsm[:, :], w=[xs.buf] + xs.bufs)
          rms_stats(lambda sub: xs.t[0:NS, 0, :], 1, NS, lambda sub: [xs.bufs[0], xs.bufs[1]], 0)
          normalize_transpose(1, NS, 0)
          S_.dma("sp", cct.t[:], cconv[:, :], w=[cct.buf])
          for c in range(8):
              k = gbank()
              S_.tr(bank(k)[:, 0:SB * 2], cct.t[0:SB * 2, c * 128:(c + 1) * 128], ident_f[0:SB * 2, 0:SB * 2],
                    r=[cct.buf, cst32.buf], w=[PB[k]])
              S_.cp("dve", ccT.t[:, c, :], bank(k)[:, 0:SB * 2], r=[PB[k]], w=[ccT.buf])
          stage(6.1)
          for kind, dst in ((4, nks), (5, nvs)):
              for hf in range(2):
                  wt = load_wblock(WB_IN + kind * 2 + hf)
                  for s in range(SB):
                      k = gbank()
                      for dc in range(8):
                          S_.mm(bank(k)[0:16, :], xnT.t[:, dc, 16 * s:16 * s + 16], wt.t[:, dc, :], dc == 0, dc == 7,
                                r=[xnT.bufs[dc], wt.buf], w=[PB[k]])
                      st = stg.next()
                      S_.cp("act", st.t[0:16, :], bank(k)[0:16, :], r=[PB[k]], w=[st.buf])
                      S_.dma("pool", dst[16 * s:16 * s + 16, hf * 512:(hf + 1) * 512], st.t[0:16, :], r=[st.buf])
                      if kind == 5:
                          S_.cp("dve", vnew.t[0:16, s, hf * 512:(hf + 1) * 512], bank(k)[0:16, :],
                                r=[PB[k]], w=[vnew.bufs[s]])
          stage(6.2)
          for c in range(8):
              w_cg = load_wchunk(1, c)
              w_hc = load_wchunk(2, c)
              w_gc = load_wchunk(6, c)
              w_bg = load_wchunk(0, c)
              w_ga = load_wchunk(7, c)
              w_q = load_wchunk(3, c)
              w_k = load_wchunk(4, c)
              k = proj_fm(w_cg, NS)
              S_.cp("act", cgs.t[:, 0:NS], bank(k)[:, 0:NS], r=[PB[k]], w=[cgs.buf])
              S_.cp("pool", us.t[:, :, 0:2], ccT.t[:, c, :].rearrange("p (s r) -> p s r", r=2),
                    r=[ccT.buf], w=[us.buf])
              k = proj_fm(w_hc, NS)
              S_.tt("dve", us.t[:, :, 2:18], bank(k)[:, 0:NS].rearrange("p (s t) -> p s t", t=16),
                    cgs.t[:, 0:NS].rearrange("p (s t) -> p s t", t=16), ALU.mult, r=[PB[k], cgs.buf], w=[us.buf])
              S_.cp("pool", carrs.t[:, c, :].rearrange("p (s r) -> p s r", r=2), us.t[:, :, 16:18],
                    r=[us.buf], w=[carrs.buf])
              tAv = tA.t[:, 0:NS].rearrange("p (s t) -> p s t", t=16)
              S_.ts("dve", tAv, us.t[:, :, 0:16], cw.t[:, c, 0:1], ALU.mult, r=[us.buf, cw.buf], w=[tA.buf])
              S_.stt("dve", tAv, us.t[:, :, 1:17], cw.t[:, c, 1:2], tAv, ALU.mult, ALU.add,
                     r=[us.buf, cw.buf, tA.buf], w=[tA.buf])
              S_.stt("dve", tAv, us.t[:, :, 2:18], cw.t[:, c, 2:3], tAv, ALU.mult, ALU.add,
                     r=[us.buf, cw.buf, tA.buf], w=[tA.buf])
              k = proj_fm(w_gc, NS)
              S_.act(sgc.t[:, 0:NS], bank(k)[:, 0:NS], AF.Sigmoid, r=[PB[k]], w=[sgc.buf])
              k = proj_fm(w_bg, NS)
              S_.tt("dve", tB.t[:, 0:NS], bank(k)[:, 0:NS], tA.t[:, 0:NS], ALU.mult, r=[PB[k], tA.buf], w=[tB.buf])
              S_.tt("pool", cvps.t[:, c, :], tB.t[:, 0:NS], sgc.t[:, 0:NS], ALU.mult,
                    r=[tB.buf, sgc.buf], w=[cvps.buf])
              k = proj_fm(w_ga, NS)
              S_.act(sgas.t[:, c, :], bank(k)[:, 0:NS], AF.Sigmoid, r=[PB[k]], w=[sgas.buf])
              k = proj_fm(w_q, NS)
              S_.ts("dve", QTs.t[:, c, :], bank(k)[:, 0:NS], HD ** -0.5, ALU.mult, r=[PB[k]], w=[QTs.buf])
              k = proj_fm(w_k, NS)
              S_.cp("act", KTn.t[:, c, :], bank(k)[:, 0:NS], r=[PB[k]], w=[KTn.buf])
          stage(6.3)
          for c in range(8):
              S_.dma("sp", ncs[:, c * 128:(c + 1) * 128].rearrange("q p -> p q"), carrs.t[:, c, :],
                     r=[carrs.buf], allow_slow_non_contiguous=True)
          stage(6.4)
          zi = 0
          OB = 4
          for s in range(SB):
              for hh in range(2):
                  S_.mm(bank(OB)[64 * hh:64 * hh + 64, 0:128], zer2.t[:, 0:64], zer2.t[:, 0:128], True, False,
                        r=[zer2.buf], w=[PB[OB]])
              S_.memset("pool", tB.t[:, 0:256], 0.0, w=[tB.buf])
              first = True
              Rb = None
              for kb in range(NKB_S, -1, -1):
                  new = (kb == NKB_S)
                  nk = 16 if new else 128
                  if new:
                      def kt_ap(c, hh, s=s):
                          return KTn.t[64 * hh:64 * hh + 64, c, 16 * s:16 * s + 16]

                      def v_ap(h, s=s):
                          return vnew.t[0:16, s, 64 * h:64 * h + 64]
                      rk = [KTn.buf]
                      rv = [vnew.bufs[s]]
                  else:
                      stK = er.next()
                      S_.dma("sp", flat(stK), ck[s, kb * 128:(kb + 1) * 128, :], w=[stK.buf])
                      kbf = spr.next()
                      S_.cp("dve", flat(kbf), flat(stK), r=[stK.buf], w=[kbf.buf])
                      ktl = rbr.next()
                      for hf2 in range(2):
                          k = gbank()
                          pT = bank(k).bitcast(BF16)
                          for c4 in range(4):
                              c = hf2 * 4 + c4
                              S_.tr(pT[:, c4 * 128:(c4 + 1) * 128], flat(kbf)[:, c * 128:(c + 1) * 128], ident_b,
                                    r=[kbf.buf, cbf.buf], w=[PB[k]])
                          S_.cp("dve", flat(ktl)[:, hf2 * 512:(hf2 + 1) * 512], pT[:, 0:512], r=[PB[k]], w=[ktl.buf])
                      stV = er.next()
                      S_.dma("sp", flat(stV), cv[s, kb * 128:(kb + 1) * 128, :], w=[stV.buf])
                      vtl = ar.next()
                      S_.cp("dve", flat(vtl), flat(stV), r=[stV.buf], w=[vtl.buf])
                      ktv = flat(ktl).rearrange("p (c n) -> p c n", c=8)
                      vfl = flat(vtl)

                      def kt_ap(c, hh, ktv=ktv):
                          return ktv[64 * hh:64 * hh + 64, c, :]

                      def v_ap(h, vfl=vfl):
                          return vfl[:, 64 * h:64 * h + 64]
                      rk = [ktl.buf]
                      rv = [vtl.buf]
                  zk = 2 * zi
                  zi = 1 - zi
                  zb = ps[:, zk * 512:(zk + 2) * 512].rearrange("p (h n) -> p h n", h=2)
                  zB = [PB[zk], PB[zk + 1]]
                  for hh in range(2):
                      S_.mm(zb[0:nk, hh, 0:128], zer2.t[:, 0:nk], zer2.t[:, 0:128], True, False,
                            r=[zer2.buf], w=[zB[hh]])
                  for c in range(8):
                      for hh in range(2):
                          S_.mm(zb[0:nk, hh, c * 16:(c + 1) * 16], kt_ap(c, hh),
                                QTs.t[64 * hh:64 * hh + 64, c, 16 * s:16 * s + 16], False, False,
                                r=rk + [QTs.buf], w=[zB[hh]])
                          if new:
                              S_.mm(zb[0:16, hh, c * 16:(c + 1) * 16], ident_b[64 * hh:64 * hh + 16, 64 * hh:64 * hh + 16],
                                    m0_b[64 * hh:64 * hh + 16, 64 * hh:64 * hh + 16],
                                    False, False, r=[cbf.buf], w=[zB[hh]])
                  ev = tA.t[:, 0:256].rearrange("p (h n) -> p h n", h=2)
                  S_.act(ev[0:nk], zb[0:nk, :, 0:128], AF.Exp, r=zB, w=[tA.buf])
                  spt = rtmp.next()
                  spv = spt.t[:, 0:256].rearrange("p (h n) -> p h n", h=2)
                  S_.act(spv[0:nk], ev[0:nk], AF.Ln, r=[tA.buf], w=[spt.buf], bias=1.0)
                  for hh in range(2):
                      S_.mm(zb[0:nk, hh, 0:128], ntri_b[0:nk, 0:nk], spv[0:nk, hh, :], False, first,
                            r=[cbf.buf, spt.buf], w=[zB[hh]])
                      if not first:
                          S_.mm(zb[0:nk, hh, 0:128], nones_b[:, 0:nk], Rb.t[:, hh * 128:(hh + 1) * 128], False, True,
                                r=[cbf.buf, Rb.buf], w=[zB[hh]])
                  at = vbf.next()
                  av = at.t[:, 0:256].rearrange("p (h n) -> p h n", h=2)
                  S_.act(av[0:nk], zb[0:nk, :, 0:128], AF.Exp, r=zB, w=[at.buf])
                  if kb > 0:
                      S_.tt("pool", tB.t[0:nk, 0:256], tB.t[0:nk, 0:256], spt.t[0:nk, 0:256], ALU.add,
                            r=[tB.buf, spt.buf], w=[tB.buf])
                      Rb = ktmp.next()
                      S_.cp("dve", Rb.t[:, 0:256], tB.t[:, 0:256], r=[tB.buf], w=[Rb.buf])
                  for c in range(8):
                      for hh in range(2):
                          h = 2 * c + hh
                          S_.mm(bank(OB)[64 * hh:64 * hh + 64, c * 16:(c + 1) * 16], v_ap(h),
                                av[0:nk, hh, c * 16:(c + 1) * 16], False, kb == 0, r=rv + [at.buf], w=[PB[OB]])
                  first = False
              ov = bank(OB)[:, 0:128].rearrange("p (c t) -> p c t", t=16)
              mt = mtmp.t[:, 0:128].rearrange("p (c t) -> p c t", t=16)
              S_.tt("dve", mt, ov, sgas.t[:, :, 16 * s:16 * s + 16], ALU.mult, r=[PB[OB], sgas.buf], w=[mtmp.buf])
              S_.tt("pool", mixT.t[:, :, 16 * s:16 * s + 16], mt, cvps.t[:, :, 16 * s:16 * s + 16], ALU.add,
                    r=[mtmp.buf, cvps.buf], w=list(mixT.bufs))
          stage(6.5)
          ffn_and_out(1, NS, lambda sub: ys[0:NS, :])
        except _Stop:
            pass
        S_.emit()
    return nc


def _consts():
    c = np.zeros((128, 640), np.float32)
    j = np.arange(128)[:, None]
    s = np.arange(128)[None, :]
    c[:, 0:128] = np.eye(128, dtype=np.float32)
    c[:, 128:256] = np.where(j >= s, -1.0, 0.0)
    c[:, 256:384] = -1.0
    c[:, 384:512] = np.where(j >= s, NEG, 0.0)
    c[:, 512:640] = np.eye(128, dtype=np.float32)
    return c


_NC_CACHE = {}


def run_cores(x_prompt, x_sample, cache_conv, cache_k, cache_v, g_mix, w_in, conv_w, w_out,
              g_ffn, w_up, w_down, g_final, n_cores):
    f = lambda a: np.ascontiguousarray(np.asarray(a, dtype=np.float32))
    x_prompt, x_sample, cache_conv, cache_k, cache_v = map(f, (x_prompt, x_sample, cache_conv, cache_k, cache_v))
    B, S, _ = x_prompt.shape
    SBT = x_sample.shape[0]
    PAST = cache_k.shape[2]
    NB = B // n_cores
    SB = SBT // n_cores
    key = (NB, S, SB, PAST)
    if key not in _NC_CACHE:
        _NC_CACHE[key] = build_nc(*key)
    nc = _NC_CACHE[key]
    shared = {
        "g_mix": f(g_mix[0]), "w_in": f(w_in[0]), "conv_w": f(conv_w[0]), "w_out": f(w_out[0]),
        "g_ffn": f(g_ffn[0]), "w_up": f(w_up[0]), "w_down": f(w_down[0]), "g_final": f(g_final),
        "consts": _consts(),
    }
    in_maps = []
    for c in range(n_cores):
        m = dict(shared)
        m["xp"] = x_prompt[c * NB:(c + 1) * NB]
        m["xsm"] = x_sample[c * SB:(c + 1) * SB].reshape(SB * DEC, D)
        m["cconv"] = cache_conv[0, c * SB:(c + 1) * SB].reshape(SB * 2, D)
        m["ck"] = cache_k[0, c * SB:(c + 1) * SB].reshape(SB, PAST, D)
        m["cv"] = cache_v[0, c * SB:(c + 1) * SB].reshape(SB, PAST, D)
        in_maps.append(m)
    res = run_bass_kernel_spmd(nc, in_maps, core_ids=list(range(n_cores)))
    R = res.results
    cat = lambda name: np.concatenate([np.asarray(r[name]) for r in R], axis=0)
    y_prompt = cat("yp").reshape(B, S, D)
    y_sample = cat("ys").reshape(SBT, DEC, D)
    ncp = cat("ncp").reshape(1, B, 2, D)
    nkp = cat("nkp").reshape(1, B, S, NH, HD)
    nvp = cat("nvp").reshape(1, B, S, NH, HD)
    ncs = cat("ncs").reshape(1, SBT, 2, D)
    nks = cat("nks").reshape(1, SBT, DEC, NH, HD)
    nvs = cat("nvs").reshape(1, SBT, DEC, NH, HD)
    return tuple(np.ascontiguousarray(a, dtype=np.float32)
                 for a in (y_prompt, y_sample, ncp, nkp, nvp, ncs, nks, nvs))


def kernel(x_prompt, x_sample, cache_conv, cache_k, cache_v, g_mix, w_in, conv_w, w_out,
           g_ffn, w_up, w_down, g_final):
    return run_cores(x_prompt, x_sample, cache_conv, cache_k, cache_v, g_mix, w_in, conv_w, w_out,
                     g_ffn, w_up, w_down, g_final, N_CORES)
```

```python
import contextlib
import numpy as np
import concourse.bass as bass
import concourse.mybir as mybir
from concourse.bass_utils import run_bass_kernel_spmd

F32 = mybir.dt.float32
BF16 = mybir.dt.bfloat16
AF = mybir.ActivationFunctionType
ALU = mybir.AluOpType
AX = mybir.AxisListType

D = 1024
NH = 16
HD = 64
DFF = 4096
NPROJ = 8192
DEC = 16
EPS = 1e-6
NEG = -30000.0
N_CORES = 8
import os
KSTAGE = float(os.environ.get('KSTAGE', '99'))
KATT = int(os.environ.get('KATT', '99'))
KSKIP = set(os.environ.get('KSKIP', '').split(','))
EPOCH = 30000


class Buf:
    __slots__ = ("name", "w", "rs", "excl")

    def __init__(self, name="", excl=False):
        self.name = name
        self.w = None
        self.rs = []
        self.excl = excl


class Op:
    __slots__ = ("eng", "fn", "deps", "dma", "sem", "val", "signal", "k")

    def __init__(self, eng, fn, dma):
        self.eng = eng
        self.fn = fn
        self.dma = dma
        self.deps = ()
        self.sem = None
        self.val = 0
        self.signal = False
        self.k = -1


class Sched:
    ENGS = ("pe", "act", "dve", "pool", "sp")

    def __init__(self, nc, es, n_dma_sems=10, n_epochs=5):
        self.nc = nc
        self.ops = {e: [] for e in self.ENGS}
        self.pending = {e: set() for e in self.ENGS}
        self.last = {e: None for e in self.ENGS}
        self.esems = {e: [es.enter_context(nc.semaphore(f"s_{e}{j}")) for j in range(n_epochs)]
                      for e in ("pe", "act", "dve", "pool")}
        self.dsems = {q: [es.enter_context(nc.semaphore(f"d_{q}{j}")) for j in range(n_dma_sems)]
                      for q in ("sp", "pool")}
        self.dcnt = {q: [0] * n_dma_sems for q in ("sp", "pool")}
        self.dlast = {q: [None] * n_dma_sems for q in ("sp", "pool")}
        self.drr = {q: 0 for q in ("sp", "pool")}
        self.all_dma = []

    def _deps(self, eng, r, w, is_dma):
        raw = set()
        other = set()
        for b in r:
            if b.w is not None:
                raw.add(b.w)
            if b.excl:
                other.update(b.rs)
        for b in w:
            if b.w is not None:
                other.add(b.w)
            other.update(b.rs)
        deps = set()
        for d in raw | other:
            if d.dma:
                deps.add(d)
            elif d.eng != eng:
                deps.add(d)
            else:
                if is_dma or eng != "pe":
                    deps.add(d)
        return deps

    def _record(self, o, r, w):
        deps = self._deps(o.eng, r, w, o.dma)
        if self.pending[o.eng]:
            deps |= self.pending[o.eng]
            self.pending[o.eng] = set()
        o.deps = tuple(deps)
        for d in deps:
            d.signal = True
        self.ops[o.eng].append(o)
        self.last[o.eng] = o
        for b in w:
            b.w = o
            b.rs = []
        for b in r:
            if b.w is not o:
                b.rs.append(o)
        return o

    def op(self, eng, fn, r=(), w=()):
        return self._record(Op(eng, fn, False), r, w)

    def dma(self, q, out, in_, r=(), w=(), **kw):
        if os.environ.get("KQ_SP", "0") == "1":
            q = "sp"
        o = Op(q, (lambda e, out=out, in_=in_, kw=kw: e.dma_start(out=out, in_=in_, **kw)), True)
        j = self.drr[q]
        self.drr[q] = (j + 1) % len(self.dsems[q])
        prev = self.dlast[q][j]
        self.dcnt[q][j] += 1
        o.sem = self.dsems[q][j]
        o.val = 16 * self.dcnt[q][j]
        self._record(o, r, w)
        if prev is not None:
            o.deps = o.deps + (prev,)
        self.dlast[q][j] = o
        self.all_dma.append(o)
        return o

    def barrier(self):
        lasts = [self.last[e] for e in ("pe", "act", "dve", "pool") if self.last[e] is not None]
        for d in lasts:
            d.signal = True
        dm = [x for q in self.dlast for x in self.dlast[q] if x is not None]
        for e in self.ENGS:
            self.pending[e] |= set(lasts) | set(dm)

    def mm(self, out, lhsT, rhs, start, stop, r, w):
        return self.op("pe", lambda e: e.matmul(out, lhsT=lhsT, rhs=rhs, start=start, stop=stop,
                                                skip_group_check=True), r, w)

    def tr(self, out, in_, ident, r, w):
        return self.op("pe", lambda e: e.transpose(out, in_, ident), r, w)

    def act(self, out, in_, func, r, w, bias=None, scale=None, accum=None):
        kw = {}
        if bias is not None:
            kw["bias"] = bias
        if scale is not None:
            kw["scale"] = scale
        if accum is not None:
            kw["accum_out"] = accum
        return self.op("act", lambda e: e.activation(out=out, in_=in_, func=func, **kw), r, w)

    def tt(self, eng, out, in0, in1, op, r, w):
        return self.op(eng, lambda e: e.tensor_tensor(out=out, in0=in0, in1=in1, op=op), r, w)

    def ts(self, eng, out, in0, s1, op0, r, w, s2=None, op1=None):
        if op1 is None:
            return self.op(eng, lambda e: e.tensor_scalar(out=out, in0=in0, scalar1=s1, scalar2=None,
                                                          op0=op0), r, w)
        return self.op(eng, lambda e: e.tensor_scalar(out=out, in0=in0, scalar1=s1, scalar2=s2,
                                                      op0=op0, op1=op1), r, w)

    def stt(self, eng, out, in0, scalar, in1, op0, op1, r, w):
        return self.op(eng, lambda e: e.scalar_tensor_tensor(out=out, in0=in0, scalar=scalar, in1=in1,
                                                             op0=op0, op1=op1), r, w)

    def cp(self, eng, out, in_, r, w):
        if eng == "act":
            return self.op("act", lambda e: e.activation(out=out, in_=in_, func=AF.Identity), r, w)
        return self.op(eng, lambda e: e.tensor_copy(out=out, in_=in_), r, w)

    def memset(self, eng, ap, val, w):
        return self.op(eng, lambda e: e.memset(ap, val), (), w)

    def recip(self, out, in_, r, w):
        return self.op("dve", lambda e: e.reciprocal(out=out, in_=in_), r, w)

    def emit(self):
        nc = self.nc
        for e in ("pe", "act", "dve", "pool"):
            k = 0
            for o in self.ops[e]:
                if not o.dma and o.signal:
                    o.k = k
                    o.sem = self.esems[e][k // EPOCH]
                    o.val = (k % EPOCH) + 1
                    k += 1
            assert k <= EPOCH * len(self.esems[e]), (e, k)

        def run(ename, eng):
            waited = {}
            for o in self.ops[ename]:
                need = {}
                for d in o.deps:
                    key = id(d.sem)
                    if key not in need or need[key][1] < d.val:
                        need[key] = (d.sem, d.val)
                for key, (sem, val) in need.items():
                    if waited.get(key, 0) >= val:
                        continue
                    eng.wait_ge(sem, val)
                    waited[key] = val
                ins = o.fn(eng)
                if o.dma:
                    ins.then_inc(o.sem, 16)
                elif o.signal:
                    ins.then_inc(o.sem, 1)
            if ename in self.dsems:
                for j, sem in enumerate(self.dsems[ename]):
                    if self.dcnt[ename][j] > 0:
                        eng.wait_ge(sem, 16 * self.dcnt[ename][j])

        with nc.Block() as block:
            @block.tensor
            def _(eng):
                run("pe", eng)

            @block.scalar
            def _(eng):
                run("act", eng)

            @block.vector
            def _(eng):
                run("dve", eng)

            @block.gpsimd
            def _(eng):
                run("pool", eng)

            @block.sync
            def _(eng):
                run("sp", eng)


class Tile:
    __slots__ = ("t", "buf", "bufs")

    def __init__(self, t, name, nb=0):
        self.t = t
        self.buf = Buf(name)
        self.bufs = [Buf(f"{name}{i}") for i in range(nb)]


class Ring:
    def __init__(self, tiles):
        self.tiles = tiles
        self.i = 0

    def next(self):
        t = self.tiles[self.i]
        self.i = (self.i + 1) % len(self.tiles)
        return t


def build_nc(NB, S, SB, PAST):
    nc = bass.Bass("TRN2", target_bir_lowering=False)
    NBLK = S // 512
    NS = SB * DEC
    NKB_S = PAST // 128

    def din(name, shape, dt=F32):
        return nc.dram_tensor(name, list(shape), dt, kind="ExternalInput").ap()

    def dout(name, shape, dt=F32):
        return nc.dram_tensor(name, list(shape), dt, kind="ExternalOutput").ap()

    xp = din("xp", [NB, S, D])
    xsm = din("xsm", [NS, D])
    cconv = din("cconv", [SB * 2, D])
    ck = din("ck", [SB, PAST, D])
    cv = din("cv", [SB, PAST, D])
    g_mix = din("g_mix", [D])
    w_in = din("w_in", [D, NPROJ])
    conv_w = din("conv_w", [3, D])
    w_out = din("w_out", [D, D])
    g_ffn = din("g_ffn", [D])
    w_up = din("w_up", [D, DFF])
    w_down = din("w_down", [DFF, D])
    g_final = din("g_final", [D])
    consts = din("consts", [128, 640])

    yp = dout("yp", [NB, S, D])
    ys = dout("ys", [NS, D])
    ncp = dout("ncp", [NB * 2, D])
    nkp = dout("nkp", [NB, S, D])
    nvp = dout("nvp", [NB, S, D])
    ncs = dout("ncs", [SB * 2, D])
    nks = dout("nks", [NS, D])
    nvs = dout("nvs", [NS, D])

    NWB = 34
    wsc = nc.dram_tensor("wsc", [NWB, 128, 4096], BF16).ap()
    kts = nc.dram_tensor("kts", [NB, 8, 128, S], BF16).ap()
    vsc = nc.dram_tensor("vsc", [NB, S, D], BF16).ap()
    WB_IN, WB_OUT, WB_UP, WB_DN = 0, 16, 18, 26

    es = contextlib.ExitStack()
    with es:
        S_ = Sched(nc, es)

        def sb(name, shape, dt, nb=0):
            return Tile(es.enter_context(nc.sbuf_tensor(name, list(shape), dt)), name, nb)

        def ring(name, shape, dt, n):
            return Ring([sb(f"{name}{i}", shape, dt) for i in range(n)])

        ps = es.enter_context(nc.psum_tensor("ps", [128, 4096], F32))
        PB = [Buf(f"bank{k}", excl=True) for k in range(8)]

        def bank(k):
            return ps[:, k * 512:(k + 1) * 512]

        gb_i = [0]
        GB = [[5, 6, 7]]

        def set_gb(lst):
            GB[0] = list(lst)
            gb_i[0] = 0

        def gbank():
            lst = GB[0]
            k = lst[gb_i[0] % len(lst)]
            gb_i[0] = (gb_i[0] + 1) % len(lst)
            return k

        cst32 = sb("cst32", [128, 640], F32)
        ident_f = cst32.t[:, 512:640]
        cbf = sb("cbf", [128, 512], BF16)
        ident_b = cbf.t[:, 0:128]
        ntri_b = cbf.t[:, 128:256]
        nones_b = cbf.t[:, 256:384]
        m0_b = cbf.t[:, 384:512]
        zer_b = sb("zer_b", [128, 64], BF16)
        cw = sb("cw", [128, 8, 3], F32)
        gmx = sb("gmx", [128, 8], F32)
        gff = sb("gff", [128, 8], F32)
        gfin = sb("gfin", [128, D], F32)
        epsb = sb("epsb", [128, 1], F32)
        oneb = sb("oneb", [128, 1], F32)

        xs = sb("xs", [128, 4, D], F32, nb=8)
        xnb = sb("xnb", [128, 4, D], BF16, nb=4)
        xnT = sb("xnT", [128, 8, 512], BF16, nb=8)
        mixT = sb("mixT", [128, 8, 512], BF16, nb=8)
        actT = sb("actT", [128, 32, 512], BF16, nb=32)
        stat = sb("stat", [128, 16], F32)
        wsl = ring("wsl", [128, 8, 512], BF16, int(os.environ.get("KWSL", "3")))
        wck = ring("wck", [128, 8, 128], BF16, 7)
        stg = ring("stg", [128, 512], F32, 2)
        vbf = ring("vbf", [128, 512], BF16, 2)
        ktmp = ring("ktmp", [128, 512], BF16, 2)
        QTr = ring("QT", [128, 512], BF16, 2)
        sgar = ring("sga", [128, 512], F32, 2)
        cvpr = ring("cvp", [128, 512], F32, 2)
        cgs = sb("cgs", [128, 512], F32)
        sgc = sb("sgc", [128, 512], F32)
        ut = ring("ut", [128, 514], F32, 2)
        tA = sb("tA", [128, 512], F32)
        tB = sb("tB", [128, 512], F32)
        carry = sb("carry", [128, 8, 2], F32, nb=8)
        ktp = ring("ktp", [128, 512], BF16, 3)
        vtp = ring("vtp", [128, 4, 128], BF16, 3)
        er = ring("e", [128, 2, 512], F32, 2)
        spr = ring("sp", [128, 2, 512], BF16, 3)
        ar = ring("a", [128, 2, 512], BF16, 2)
        R32 = sb("R32", [128, 2, 512], F32)
        rbr = ring("Rb", [128, 2, 512], BF16, 3)
        mtmp = sb("mtmp", [128, 512], F32)
        rtmp = ring("rtmp", [128, 512], BF16, 2)

        QTs = sb("QTs", [128, 8, NS], BF16)
        KTn = sb("KTn", [128, 8, NS], BF16)
        sgas = sb("sgas", [128, 8, NS], F32)
        cvps = sb("cvps", [128, 8, NS], F32)
        us = sb("us", [128, SB, 18], F32)
        carrs = sb("carrs", [128, 8, SB * 2], F32)
        cct = sb("cct", [SB * 2, D], F32)
        ccT = sb("ccT", [128, 8, SB * 2], F32)
        vnew = sb("vnew", [16, SB, D], BF16, nb=SB)
        zer2 = sb("zer2", [128, 256], BF16)

        if os.environ.get('KDBG'):
            print('SBUF remaining bytes/partition:', nc.sbuf_bytes_remaining)
        KTS = [[[Buf() for _ in range(NBLK)] for _ in range(8)] for _ in range(NB)]
        VSB = [[[Buf() for _ in range(2)] for _ in range(S // 128)] for _ in range(NB)]
        WSC = [Buf() for _ in range(NWB)]

        S_.dma("sp", cst32.t[:], consts[:, :], w=[cst32.buf])
        S_.cp("dve", cbf.t[:], cst32.t[:, 0:512], r=[cst32.buf], w=[cbf.buf])
        S_.memset("pool", zer_b.t[:], 0.0, w=[zer_b.buf])
        S_.memset("pool", zer2.t[:], 0.0, w=[zer2.buf])
        S_.memset("pool", epsb.t[:], EPS, w=[epsb.buf])
        S_.memset("pool", oneb.t[:], 1.0, w=[oneb.buf])
        for tap in range(3):
            S_.dma("sp", cw.t[:, :, tap], conv_w[tap].rearrange("(c p) -> p c", p=128), w=[cw.buf],
                   allow_slow_non_contiguous=True)
        S_.dma("sp", gmx.t[:], g_mix.rearrange("(c p) -> p c", p=128), w=[gmx.buf],
               allow_slow_non_contiguous=True)
        S_.dma("sp", gff.t[:], g_ffn.rearrange("(c p) -> p c", p=128), w=[gff.buf],
               allow_slow_non_contiguous=True)
        S_.dma("sp", gfin.t[:], g_final.partition_broadcast(128), w=[gfin.buf])

        stage_views = [
            (xs.t[:].rearrange("p a d -> p (a d)").rearrange("p (c n) -> p c n", c=8), [xs.buf] + xs.bufs),
            (actT.t[:, 0:16, :].rearrange("p a n -> p (a n)").bitcast(F32).rearrange("p (c n) -> p c n", c=8),
             actT.bufs[0:16]),
            (actT.t[:, 16:32, :].rearrange("p a n -> p (a n)").bitcast(F32).rearrange("p (c n) -> p c n", c=8),
             actT.bufs[16:32]),
        ]
        wblocks = []
        for j in range(16):
            wblocks.append((WB_IN + j, w_in, 0, j * 512, gmx))
        for j in range(2):
            wblocks.append((WB_OUT + j, w_out, 0, j * 512, None))
        for j in range(8):
            wblocks.append((WB_UP + j, w_up, 0, j * 512, gff))
        for g in range(4):
            for hf in range(2):
                wblocks.append((WB_DN + g * 2 + hf, w_down, g * 1024, hf * 512, None))
        for n, (bi, W, r0, n0, gain) in enumerate(wblocks):
            sv, sbufs = stage_views[n % 3]
            S_.dma("sp", sv, W[r0:r0 + 1024, n0:n0 + 512].rearrange("(c p) n -> p c n", p=128), w=sbufs)
            wt = wsl.next()
            eng = "dve" if n % 2 == 0 else "act"
            if gain is None:
                S_.cp(eng, wt.t[:], sv, r=sbufs, w=[wt.buf])
            else:
                for c in range(8):
                    if eng == "dve":
                        S_.ts(eng, wt.t[:, c, :], sv[:, c, :], gain.t[:, c:c + 1], ALU.mult,
                              r=sbufs + [gain.buf], w=[wt.buf])
                    else:
                        S_.act(wt.t[:, c, :], sv[:, c, :], AF.Identity, r=sbufs + [gain.buf], w=[wt.buf],
                               scale=gain.t[:, c:c + 1])
            S_.dma("sp", wsc[bi].rearrange("p (c n) -> p c n", c=8), wt.t[:], r=[wt.buf], w=[WSC[bi]])

        def rms_stats(src_ap_fn, nsub, npart, rbufs, col0):
            S_.memset("pool", stat.t[:, col0:col0 + nsub], 0.0, w=[stat.buf])
            for sub in range(nsub):
                S_.act(xnb.t[0:npart, sub, :], src_ap_fn(sub), AF.Square, r=rbufs(sub) + [stat.buf],
                       w=[xnb.bufs[sub], stat.buf], accum=stat.t[0:npart, col0 + sub:col0 + sub + 1])
            sl = stat.t[0:npart, col0:col0 + nsub]
            S_.act(sl, sl, AF.Sqrt, r=[stat.buf, epsb.buf], w=[stat.buf], bias=epsb.t[0:npart, :], scale=1.0 / D)
            S_.recip(sl, sl, r=[stat.buf], w=[stat.buf])

        def normalize_transpose(nsub, npart, col0):
            ntok = nsub * npart
            for sub in range(nsub):
                S_.ts("dve", xnb.t[0:npart, sub, :], xs.t[0:npart, sub, :],
                      stat.t[0:npart, col0 + sub:col0 + sub + 1], ALU.mult,
                      r=[xs.bufs[sub * 2], xs.bufs[sub * 2 + 1], stat.buf], w=[xnb.bufs[sub]])
            for dc in range(8):
                k = gbank()
                pT = bank(k).bitcast(BF16)
                for sub in range(nsub):
                    S_.tr(pT[:, sub * npart:(sub + 1) * npart], xnb.t[0:npart, sub, dc * 128:(dc + 1) * 128],
                          ident_b[0:npart, 0:npart], r=[xnb.bufs[sub], cbf.buf], w=[PB[k]])
                if os.environ.get("KT_NOCOPY") == "1":
                    continue
                S_.cp("act" if (dc % 2 == 1 and os.environ.get("KT_ACT") == "1") else "dve",
                      xnT.t[:, dc, 0:ntok], pT[:, 0:ntok], r=[PB[k]], w=[xnT.bufs[dc]])

        def load_wblock(bi):
            wt = wsl.next()
            S_.dma("sp", wt.t[:], wsc[bi].rearrange("p (c n) -> p c n", c=8), r=[WSC[bi]], w=[wt.buf])
            return wt

        def load_wchunk(kind, c):
            bi = WB_IN + kind * 2 + c // 4
            wt = wck.next()
            src = wsc[bi].rearrange("p (c n) -> p c n", c=8)[:, :, (c % 4) * 128:(c % 4) * 128 + 128]
            S_.dma("sp", wt.t[:], src, r=[WSC[bi]], w=[wt.buf])
            return wt

        def proj_fm(wt, ntok):
            k = gbank()
            for dc in range(8):
                S_.mm(bank(k)[:, 0:ntok], wt.t[:, dc, :], xnT.t[:, dc, 0:ntok], dc == 0, dc == 7,
                      r=[wt.buf, xnT.bufs[dc]], w=[PB[k]])
            return k

        xnT_all = list(xnT.bufs)

        def ffn_and_out(nsub, npart, y_dst_fn):
            ntok = nsub * npart
            set_gb([4, 5, 6, 7])
            for hf in range(2):
                wt = load_wblock(WB_OUT + hf)
                for sub in range(nsub):
                    k = gbank()
                    for c in range(8):
                        S_.mm(bank(k)[0:npart, :], mixT.t[:, c, sub * npart:(sub + 1) * npart], wt.t[:, c, :],
                              c == 0, c == 7, r=[mixT.bufs[c], wt.buf], w=[PB[k]])
                    hb = xs.bufs[sub * 2 + hf]
                    S_.tt("dve", xs.t[0:npart, sub, hf * 512:(hf + 1) * 512],
                          xs.t[0:npart, sub, hf * 512:(hf + 1) * 512], bank(k)[0:npart, :], ALU.add,
                          r=[hb, PB[k]], w=[hb])
            rms_stats(lambda sub: xs.t[0:npart, sub, :], nsub, npart,
                      lambda sub: [xs.bufs[sub * 2], xs.bufs[sub * 2 + 1]], 4)
            normalize_transpose(nsub, npart, 4)
            for s in range(8):
                wt = load_wblock(WB_UP + s)
                for j in range(4):
                    fk = s * 4 + j
                    k = gbank()
                    for dc in range(8):
                        S_.mm(bank(k)[:, 0:ntok], wt.t[:, dc, j * 128:(j + 1) * 128], xnT.t[:, dc, 0:ntok],
                              dc == 0, dc == 7, r=[wt.buf, xnT.bufs[dc]], w=[PB[k]])
                    rt = rtmp.next()
                    S_.act(rt.t[:, 0:ntok], bank(k)[:, 0:ntok], AF.Relu, r=[PB[k]], w=[rt.buf])
                    S_.tt("pool", actT.t[:, fk, 0:ntok], rt.t[:, 0:ntok], rt.t[:, 0:ntok], ALU.mult,
                          r=[rt.buf], w=[actT.bufs[fk]])
            for hf in range(2):
                for g in range(4):
                    wt = load_wblock(WB_DN + g * 2 + hf)
                    for sub in range(nsub):
                        for fkk in range(8):
                            fk = g * 8 + fkk
                            S_.mm(bank(sub)[0:npart, :], actT.t[:, fk, sub * npart:(sub + 1) * npart],
                                  wt.t[:, fkk, :], g == 0 and fkk == 0, g == 3 and fkk == 7,
                                  r=[actT.bufs[fk], wt.buf], w=[PB[sub]])
                for sub in range(nsub):
                    hb = xs.bufs[sub * 2 + hf]
                    S_.tt("dve", xs.t[0:npart, sub, hf * 512:(hf + 1) * 512],
                          xs.t[0:npart, sub, hf * 512:(hf + 1) * 512], bank(sub)[0:npart, :], ALU.add,
                          r=[hb, PB[sub]], w=[hb])
            set_gb([5, 6, 7])
            rms_stats(lambda sub: xs.t[0:npart, sub, :], nsub, npart,
                      lambda sub: [xs.bufs[sub * 2], xs.bufs[sub * 2 + 1]], 8)
            for sub in range(nsub):
                yv = actT.t[:, 4 * sub:4 * sub + 4, :].rearrange("p a n -> p (a n)").bitcast(F32)
                yb = list(actT.bufs[4 * sub:4 * sub + 4])
                S_.stt("dve", yv[0:npart, :], xs.t[0:npart, sub, :], stat.t[0:npart, 8 + sub:9 + sub],
                       gfin.t[0:npart, :], ALU.mult, ALU.mult,
                       r=[xs.bufs[sub * 2], xs.bufs[sub * 2 + 1], stat.buf, gfin.buf], w=yb)
                S_.dma("pool", y_dst_fn(sub), yv[0:npart, :], r=yb)

        class _Stop(Exception):
            pass

        def stage(n):
            if KSTAGE <= n:
                raise _Stop()

        try:
          stage(0)
          for b in range(NB):
            for i in range(NBLK):
                t0 = i * 512
                for sub in range(4):
                    S_.dma("sp", xs.t[:, sub, :], xp[b, t0 + sub * 128:t0 + (sub + 1) * 128, :],
                           w=[xs.bufs[2 * sub], xs.bufs[2 * sub + 1]])
                set_gb([5, 6, 7, 0, 1, 2, 3, 4])
                rms_stats(lambda sub: xs.t[:, sub, :], 4, 128,
                          lambda sub: [xs.bufs[sub * 2], xs.bufs[sub * 2 + 1]], 0)
                stage(0.5)
                normalize_transpose(4, 128, 0)
                stage(1)
                for kind, dst in ((4, nkp), (5, nvp)):
                    for hf in range(2):
                        wt = load_wblock(WB_IN + kind * 2 + hf)
                        for sub in range(4):
                            k = gbank()
                            for dc in range(8):
                                S_.mm(bank(k), xnT.t[:, dc, sub * 128:(sub + 1) * 128], wt.t[:, dc, :],
                                      dc == 0, dc == 7, r=[xnT.bufs[dc], wt.buf], w=[PB[k]])
                            st = stg.next()
                            S_.cp("act", st.t[:], bank(k), r=[PB[k]], w=[st.buf])
                            S_.dma("pool", dst[b, t0 + sub * 128:t0 + (sub + 1) * 128, hf * 512:(hf + 1) * 512],
                                   st.t[:], r=[st.buf])
                            if kind == 5 and os.environ.get("K_NOVSC") != "1":
                                vb = vbf.next()
                                S_.cp("dve", vb.t[:], bank(k), r=[PB[k]], w=[vb.buf])
                                kb = i * 4 + sub
                                if os.environ.get("K_NOVDMA") != "1":
                                  S_.dma("pool", vsc[b, t0 + sub * 128:t0 + (sub + 1) * 128, hf * 512:(hf + 1) * 512],
                                       vb.t[:], r=[vb.buf], w=[VSB[b][kb][hf]])
                stage(2)
                set_gb([5, 6, 7])
                pend = None
                for c in range(8 + 1):
                    if c < 8:
                        w_cg = load_wchunk(1, c)
                        w_hc = load_wchunk(2, c)
                        w_gc = load_wchunk(6, c)
                        w_bg = load_wchunk(0, c)
                        w_ga = load_wchunk(7, c)
                        w_q = load_wchunk(3, c)
                        w_k = load_wchunk(4, c)
                        k = proj_fm(w_cg, 512)
                        S_.cp("act", cgs.t[:], bank(k), r=[PB[k]], w=[cgs.buf])
                        u = ut.next()
                        if i == 0:
                            S_.memset("pool", u.t[:, 0:2], 0.0, w=[u.buf])
                        else:
                            S_.cp("pool", u.t[:, 0:2], carry.t[:, c, :], r=[carry.bufs[c]], w=[u.buf])
                        k = proj_fm(w_hc, 512)
                        S_.tt("dve", u.t[:, 2:514], bank(k), cgs.t[:], ALU.mult, r=[PB[k], cgs.buf], w=[u.buf])
                        S_.cp("pool", carry.t[:, c, :], u.t[:, 512:514], r=[u.buf], w=[carry.bufs[c]])
                        S_.ts("dve", tA.t[:], u.t[:, 0:512], cw.t[:, c, 0:1], ALU.mult, r=[u.buf, cw.buf], w=[tA.buf])
                        S_.stt("dve", tA.t[:], u.t[:, 1:513], cw.t[:, c, 1:2], tA.t[:], ALU.mult, ALU.add,
                               r=[u.buf, cw.buf, tA.buf], w=[tA.buf])
                        S_.stt("dve", tA.t[:], u.t[:, 2:514], cw.t[:, c, 2:3], tA.t[:], ALU.mult, ALU.add,
                               r=[u.buf, cw.buf, tA.buf], w=[tA.buf])
                        k = proj_fm(w_gc, 512)
                        S_.act(sgc.t[:], bank(k), AF.Sigmoid, r=[PB[k]], w=[sgc.buf])
                        k = proj_fm(w_bg, 512)
                        S_.tt("dve", tB.t[:], bank(k), tA.t[:], ALU.mult, r=[PB[k], tA.buf], w=[tB.buf])
                        cvp = cvpr.next()
                        S_.tt("pool", cvp.t[:], tB.t[:], sgc.t[:], ALU.mult, r=[tB.buf, sgc.buf], w=[cvp.buf])
                        k = proj_fm(w_ga, 512)
                        sga = sgar.next()
                        S_.act(sga.t[:], bank(k), AF.Sigmoid, r=[PB[k]], w=[sga.buf])
                        k = proj_fm(w_q, 512)
                        QT = QTr.next()
                        S_.ts("dve", QT.t[:], bank(k), HD ** -0.5, ALU.mult, r=[PB[k]], w=[QT.buf])
                        k = proj_fm(w_k, 512)
                        kt = ktmp.next()
                        S_.cp("act", kt.t[:], bank(k), r=[PB[k]], w=[kt.buf])
                        S_.dma("pool", kts[b, c, :, t0:t0 + 512], kt.t[:], r=[kt.buf], w=[KTS[b][c][i]])
                        cur = (c, QT, sga, cvp)
                    else:
                        cur = None
                    if pend is not None and KSTAGE > 3:
                        pc, pQT, psga, pcvp = pend
                        OB = 4
                        for hh in range(2):
                            S_.mm(bank(OB)[64 * hh:64 * hh + 64, :], zer_b.t[:, 0:64], pQT.t[:, :], True, False,
                                  r=[zer_b.buf, pQT.buf], w=[PB[OB]])
                        S_.memset("pool", R32.t[:], 0.0, w=[R32.buf])
                        tiles = [(g, m) for g in range(i, -1, -1) for m in (3, 2, 1, 0)]
                        T = len(tiles)
                        st = [None] * T
                        pieces = {}
                        ZR = (0, 2, 5)

                        def stageA(t, b=b, pc=pc, pQT=pQT, i=i, tiles=tiles, st=st, pieces=pieces):
                            g, m = tiles[t]
                            if g not in pieces:
                                KTp = ktp.next()
                                S_.dma("sp", KTp.t[:], kts[b, pc, :, g * 512:(g + 1) * 512], r=[KTS[b][pc][g]],
                                       w=[KTp.buf])
                                Vp = vtp.next()
                                S_.dma("sp", Vp.t[:], vsc[b, g * 512:(g + 1) * 512, pc * 128:(pc + 1) * 128]
                                       .rearrange("(m p) n -> p m n", p=128),
                                       r=[VSB[b][4 * g + mm][pc // 4] for mm in range(4)], w=[Vp.buf])
                                pieces[g] = (KTp, Vp)
                            KTp, Vp = pieces[g]
                            kb = 4 * g + m
                            diag = (g == i)
                            c0 = 128 * m if diag else 0
                            zk = ZR[t % 3]
                            zb = ps[:, zk * 512:(zk + 2) * 512].rearrange("p (h n) -> p h n", h=2)
                            zB = [PB[zk], PB[zk + 1]]
                            for hh in range(2):
                                S_.mm(zb[:, hh, c0:512], KTp.t[64 * hh:64 * hh + 64, m * 128:(m + 1) * 128],
                                      pQT.t[64 * hh:64 * hh + 64, c0:512], True, not diag,
                                      r=[KTp.buf, pQT.buf], w=[zB[hh]])
                                if diag:
                                    S_.mm(zb[:, hh, c0:c0 + 128], ident_b, m0_b, False, True,
                                          r=[cbf.buf], w=[zB[hh]])
                            e = er.next()
                            S_.act(e.t[:, :, c0:512], zb[:, :, c0:512], AF.Exp, r=zB, w=[e.buf])
                            sp_ = spr.next()
                            S_.act(sp_.t[:, :, c0:512], e.t[:, :, c0:512], AF.Ln, r=[e.buf],
                                   w=[sp_.buf], bias=1.0)
                            Rb = None
                            if kb > 0:
                                S_.tt("dve", R32.t[:, :, c0:512], R32.t[:, :, c0:512], sp_.t[:, :, c0:512],
                                      ALU.add, r=[R32.buf, sp_.buf], w=[R32.buf])
                                Rb = rbr.next()
                                S_.cp("dve", Rb.t[:], R32.t[:], r=[R32.buf], w=[Rb.buf])
                            st[t] = dict(zb=zb, zB=zB, c0=c0, sp=sp_, Rb=Rb, Vp=Vp, m=m, kb=kb)

                        def stageB(t, st=st):
                            d = st[t]
                            zb, zB, c0, sp_ = d["zb"], d["zB"], d["c0"], d["sp"]
                            first = (t == 0)
                            for hh in range(2):
                                S_.mm(zb[:, hh, c0:512], ntri_b, sp_.t[:, hh, c0:512], False, first,
                                      r=[cbf.buf, sp_.buf], w=[zB[hh]])
                                if not first:
                                    Rb = st[t - 1]["Rb"]
                                    S_.mm(zb[:, hh, c0:512], nones_b, Rb.t[:, hh, c0:512], False, True,
                                          r=[cbf.buf, Rb.buf], w=[zB[hh]])
                            a = ar.next()
                            S_.act(a.t[:, :, c0:512], zb[:, :, c0:512], AF.Exp, r=zB, w=[a.buf])
                            d["a"] = a

                        def stageC(t, st=st):
                            d = st[t]
                            a, c0, Vp, m = d["a"], d["c0"], d["Vp"], d["m"]
                            for hh in range(2):
                                S_.mm(bank(OB)[64 * hh:64 * hh + 64, c0:512], Vp.t[:, m, 64 * hh:64 * hh + 64],
                                      a.t[:, hh, c0:512], False, d["kb"] == 0,
                                      r=[Vp.buf, a.buf], w=[PB[OB]])

                        for t in range(T + 2):
                            if t < T:
                                stageA(t)
                            if 0 <= t - 1 < T:
                                stageB(t - 1)
                            if 0 <= t - 2 < T:
                                stageC(t - 2)
                        S_.tt("dve", mtmp.t[:], bank(OB), psga.t[:], ALU.mult, r=[PB[OB], psga.buf], w=[mtmp.buf])
                        S_.tt("pool", mixT.t[:, pc, :], mtmp.t[:], pcvp.t[:], ALU.add,
                              r=[mtmp.buf, pcvp.buf], w=[mixT.bufs[pc]])
                    pend = cur
                stage(4)
                if i == NBLK - 1:
                    for c in range(8):
                        S_.dma("sp", ncp[2 * b:2 * b + 2, c * 128:(c + 1) * 128].rearrange("q p -> p q"),
                               carry.t[:, c, :], r=[carry.bufs[c]], allow_slow_non_contiguous=True)
                stage(5)
                ffn_and_out(4, 128, lambda sub, b=b, t0=t0: yp[b, t0 + sub * 128:t0 + (sub + 1) * 128, :])


          stage(6)

          def flat(t):
              return t.t[:].rearrange("p a n -> p (a n)")

          S_.dma("sp", xs.t[0:NS, 0, :], xsm[:, :], w=[xs.buf] + xs.bufs)
          rms_stats(lambda sub: xs.t[0:NS, 0, :], 1, NS, lambda sub: [xs.bufs[0], xs.bufs[1]], 0)
          normalize_transpose(1, NS, 0)
          S_.dma("sp", cct.t[:], cconv[:, :], w=[cct.buf])
          for c in range(8):
              k = gbank()
              S_.tr(bank(k)[:, 0:SB * 2], cct.t[0:SB * 2, c * 128:(c + 1) * 128], ident_f[0:SB * 2, 0:SB * 2],
                    r=[cct.buf, cst32.buf], w=[PB[k]])
              S_.cp("dve", ccT.t[:, c, :], bank(k)[:, 0:SB * 2], r=[PB[k]], w=[ccT.buf])
          stage(6.1)
          for kind, dst in ((4, nks), (5, nvs)):
              for hf in range(2):
                  wt = load_wblock(WB_IN + kind * 2 + hf)
                  for s in range(SB):
                      k = gbank()
                      for dc in range(8):
                          S_.mm(bank(k)[0:16, :], xnT.t[:, dc, 16 * s:16 * s + 16], wt.t[:, dc, :], dc == 0, dc == 7,
                                r=[xnT.bufs[dc], wt.buf], w=[PB[k]])
                      st = stg.next()
                      S_.cp("act", st.t[0:16, :], bank(k)[0:16, :], r=[PB[k]], w=[st.buf])
                      S_.dma("pool", dst[16 * s:16 * s + 16, hf * 512:(hf + 1) * 512], st.t[0:16, :], r=[st.buf])
                      if kind == 5:
                          S_.cp("dve", vnew.t[0:16, s, hf * 512:(hf + 1) * 512], bank(k)[0:16, :],
                                r=[PB[k]], w=[vnew.bufs[s]])
          stage(6.2)
          for c in range(8):
              w_cg = load_wchunk(1, c)
              w_hc = load_wchunk(2, c)
              w_gc = load_wchunk(6, c)
              w_bg = load_wchunk(0, c)
              w_ga = load_wchunk(7, c)
              w_q = load_wchunk(3, c)
              w_k = load_wchunk(4, c)
              k = proj_fm(w_cg, NS)
              S_.cp("act", cgs.t[:, 0:NS], bank(k)[:, 0:NS], r=[PB[k]], w=[cgs.buf])
              S_.cp("pool", us.t[:, :, 0:2], ccT.t[:, c, :].rearrange("p (s r) -> p s r", r=2),
                    r=[ccT.buf], w=[us.buf])
              k = proj_fm(w_hc, NS)
              S_.tt("dve", us.t[:, :, 2:18], bank(k)[:, 0:NS].rearrange("p (s t) -> p s t", t=16),
                    cgs.t[:, 0:NS].rearrange("p (s t) -> p s t", t=16), ALU.mult, r=[PB[k], cgs.buf], w=[us.buf])
              S_.cp("pool", carrs.t[:, c, :].rearrange("p (s r) -> p s r", r=2), us.t[:, :, 16:18],
                    r=[us.buf], w=[carrs.buf])
              tAv = tA.t[:, 0:NS].rearrange("p (s t) -> p s t", t=16)
              S_.ts("dve", tAv, us.t[:, :, 0:16], cw.t[:, c, 0:1], ALU.mult, r=[us.buf, cw.buf], w=[tA.buf])
              S_.stt("dve", tAv, us.t[:, :, 1:17], cw.t[:, c, 1:2], tAv, ALU.mult, ALU.add,
                     r=[us.buf, cw.buf, tA.buf], w=[tA.buf])
              S_.stt("dve", tAv, us.t[:, :, 2:18], cw.t[:, c, 2:3], tAv, ALU.mult, ALU.add,
                     r=[us.buf, cw.buf, tA.buf], w=[tA.buf])
              k = proj_fm(w_gc, NS)
              S_.act(sgc.t[:, 0:NS], bank(k)[:, 0:NS], AF.Sigmoid, r=[PB[k]], w=[sgc.buf])
              k = proj_fm(w_bg, NS)
              S_.tt("dve", tB.t[:, 0:NS], bank(k)[:, 0:NS], tA.t[:, 0:NS], ALU.mult, r=[PB[k], tA.buf], w=[tB.buf])
              S_.tt("pool", cvps.t[:, c, :], tB.t[:, 0:NS], sgc.t[:, 0:NS], ALU.mult,
                    r=[tB.buf, sgc.buf], w=[cvps.buf])
              k = proj_fm(w_ga, NS)
              S_.act(sgas.t[:, c, :], bank(k)[:, 0:NS], AF.Sigmoid, r=[PB[k]], w=[sgas.buf])
              k = proj_fm(w_q, NS)
              S_.ts("dve", QTs.t[:, c, :], bank(k)[:, 0:NS], HD ** -0.5, ALU.mult, r=[PB[k]], w=[QTs.buf])
              k = proj_fm(w_k, NS)
              S_.cp("act", KTn.t[:, c, :], bank(k)[:, 0:NS], r=[PB[k]], w=[KTn.buf])
          stage(6.3)
          for c in range(8):
              S_.dma("sp", ncs[:, c * 128:(c + 1) * 128].rearrange("q p -> p q"), carrs.t[:, c, :],
                     r=[carrs.buf], allow_slow_non_contiguous=True)
          stage(6.4)
          zi = 0
          OB = 4
          for s in range(SB):
              for hh in range(2):
                  S_.mm(bank(OB)[64 * hh:64 * hh + 64, 0:128], zer2.t[:, 0:64], zer2.t[:, 0:128], True, False,
                        r=[zer2.buf], w=[PB[OB]])
              S_.memset("pool", tB.t[:, 0:256], 0.0, w=[tB.buf])
              first = True
              Rb = None
              for kb in range(NKB_S, -1, -1):
                  new = (kb == NKB_S)
                  nk = 16 if new else 128
                  if new:
                      def kt_ap(c, hh, s=s):
                          return KTn.t[64 * hh:64 * hh + 64, c, 16 * s:16 * s + 16]

                      def v_ap(h, s=s):
                          return vnew.t[0:16, s, 64 * h:64 * h + 64]
                      rk = [KTn.buf]
                      rv = [vnew.bufs[s]]
                  else:
                      stK = er.next()
                      S_.dma("sp", flat(stK), ck[s, kb * 128:(kb + 1) * 128, :], w=[stK.buf])
                      kbf = spr.next()
                      S_.cp("dve", flat(kbf), flat(stK), r=[stK.buf], w=[kbf.buf])
                      ktl = rbr.next()
                      for hf2 in range(2):
                          k = gbank()
                          pT = bank(k).bitcast(BF16)
                          for c4 in range(4):
                              c = hf2 * 4 + c4
                              S_.tr(pT[:, c4 * 128:(c4 + 1) * 128], flat(kbf)[:, c * 128:(c + 1) * 128], ident_b,
                                    r=[kbf.buf, cbf.buf], w=[PB[k]])
                          S_.cp("dve", flat(ktl)[:, hf2 * 512:(hf2 + 1) * 512], pT[:, 0:512], r=[PB[k]], w=[ktl.buf])
                      stV = er.next()
                      S_.dma("sp", flat(stV), cv[s, kb * 128:(kb + 1) * 128, :], w=[stV.buf])
                      vtl = ar.next()
                      S_.cp("dve", flat(vtl), flat(stV), r=[stV.buf], w=[vtl.buf])
                      ktv = flat(ktl).rearrange("p (c n) -> p c n", c=8)
                      vfl = flat(vtl)

                      def kt_ap(c, hh, ktv=ktv):
                          return ktv[64 * hh:64 * hh + 64, c, :]

                      def v_ap(h, vfl=vfl):
                          return vfl[:, 64 * h:64 * h + 64]
                      rk = [ktl.buf]
                      rv = [vtl.buf]
                  zk = 2 * zi
                  zi = 1 - zi
                  zb = ps[:, zk * 512:(zk + 2) * 512].rearrange("p (h n) -> p h n", h=2)
                  zB = [PB[zk], PB[zk + 1]]
                  for hh in range(2):
                      S_.mm(zb[0:nk, hh, 0:128], zer2.t[:, 0:nk], zer2.t[:, 0:128], True, False,
                            r=[zer2.buf], w=[zB[hh]])
                  for c in range(8):
                      for hh in range(2):
                          S_.mm(zb[0:nk, hh, c * 16:(c + 1) * 16], kt_ap(c, hh),
                                QTs.t[64 * hh:64 * hh + 64, c, 16 * s:16 * s + 16], False, False,
                                r=rk + [QTs.buf], w=[zB[hh]])
                          if new:
                              S_.mm(zb[0:16, hh, c * 16:(c + 1) * 16], ident_b[64 * hh:64 * hh + 16, 64 * hh:64 * hh + 16],
                                    m0_b[64 * hh:64 * hh + 16, 64 * hh:64 * hh + 16],
                                    False, False, r=[cbf.buf], w=[zB[hh]])
                  ev = tA.t[:, 0:256].rearrange("p (h n) -> p h n", h=2)
                  S_.act(ev[0:nk], zb[0:nk, :, 0:128], AF.Exp, r=zB, w=[tA.buf])
                  spt = rtmp.next()
                  spv = spt.t[:, 0:256].rearrange("p (h n) -> p h n", h=2)
                  S_.act(spv[0:nk], ev[0:nk], AF.Ln, r=[tA.buf], w=[spt.buf], bias=1.0)
                  for hh in range(2):
                      S_.mm(zb[0:nk, hh, 0:128], ntri_b[0:nk, 0:nk], spv[0:nk, hh, :], False, first,
                            r=[cbf.buf, spt.buf], w=[zB[hh]])
                      if not first:
                          S_.mm(zb[0:nk, hh, 0:128], nones_b[:, 0:nk], Rb.t[:, hh * 128:(hh + 1) * 128], False, True,
                                r=[cbf.buf, Rb.buf], w=[zB[hh]])
                  at = vbf.next()
                  av = at.t[:, 0:256].rearrange("p (h n) -> p h n", h=2)
                  S_.act(av[0:nk], zb[0:nk, :, 0:128], AF.Exp, r=zB, w=[at.buf])
                  if kb > 0:
                      S_.tt("pool", tB.t[0:nk, 0:256], tB.t[0:nk, 0:256], spt.t[0:nk, 0:256], ALU.add,
                            r=[tB.buf, spt.buf], w=[tB.buf])
                      Rb = ktmp.next()
                      S_.cp("dve", Rb.t[:, 0:256], tB.t[:, 0:256], r=[tB.buf], w=[Rb.buf])
                  for c in range(8):
                      for hh in range(2):
                          h = 2 * c + hh
                          S_.mm(bank(OB)[64 * hh:64 * hh + 64, c * 16:(c + 1) * 16], v_ap(h),
                                av[0:nk, hh, c * 16:(c + 1) * 16], False, kb == 0, r=rv + [at.buf], w=[PB[OB]])
                  first = False
              ov = bank(OB)[:, 0:128].rearrange("p (c t) -> p c t", t=16)
              mt = mtmp.t[:, 0:128].rearrange("p (c t) -> p c t", t=16)
              S_.tt("dve", mt, ov, sgas.t[:, :, 16 * s:16 * s + 16], ALU.mult, r=[PB[OB], sgas.buf], w=[mtmp.buf])
              S_.tt("pool", mixT.t[:, :, 16 * s:16 * s + 16], mt, cvps.t[:, :, 16 * s:16 * s + 16], ALU.add,
                    r=[mtmp.buf, cvps.buf], w=list(mixT.bufs))
          stage(6.5)
          ffn_and_out(1, NS, lambda sub: ys[0:NS, :])
        except _Stop:
            pass
        S_.emit()
    return nc


def _consts():
    c = np.zeros((128, 640), np.float32)
    j = np.arange(128)[:, None]
    s = np.arange(128)[None, :]
    c[:, 0:128] = np.eye(128, dtype=np.float32)
    c[:, 128:256] = np.where(j >= s, -1.0, 0.0)
    c[:, 256:384] = -1.0
    c[:, 384:512] = np.where(j >= s, NEG, 0.0)
    c[:, 512:640] = np.eye(128, dtype=np.float32)
    return c


_NC_CACHE = {}


def run_cores(x_prompt, x_sample, cache_conv, cache_k, cache_v, g_mix, w_in, conv_w, w_out,
              g_ffn, w_up, w_down, g_final, n_cores):
    f = lambda a: np.ascontiguousarray(np.asarray(a, dtype=np.float32))
    x_prompt, x_sample, cache_conv, cache_k, cache_v = map(f, (x_prompt, x_sample, cache_conv, cache_k, cache_v))
    B, S, _ = x_prompt.shape
    SBT = x_sample.shape[0]
    PAST = cache_k.shape[2]
    NB = B // n_cores
    SB = SBT // n_cores
    key = (NB, S, SB, PAST)
    if key not in _NC_CACHE:
        _NC_CACHE[key] = build_nc(*key)
    nc = _NC_CACHE[key]
    shared = {
        "g_mix": f(g_mix[0]), "w_in": f(w_in[0]), "conv_w": f(conv_w[0]), "w_out": f(w_out[0]),
        "g_ffn": f(g_ffn[0]), "w_up": f(w_up[0]), "w_down": f(w_down[0]), "g_final": f(g_final),
        "consts": _consts(),
    }
    in_maps = []
    for c in range(n_cores):
        m = dict(shared)
        m["xp"] = x_prompt[c * NB:(c + 1) * NB]
        m["xsm"] = x_sample[c * SB:(c + 1) * SB].reshape(SB * DEC, D)
        m["cconv"] = cache_conv[0, c * SB:(c + 1) * SB].reshape(SB * 2, D)
        m["ck"] = cache_k[0, c * SB:(c + 1) * SB].reshape(SB, PAST, D)
        m["cv"] = cache_v[0, c * SB:(c + 1) * SB].reshape(SB, PAST, D)
        in_maps.append(m)
    res = run_bass_kernel_spmd(nc, in_maps, core_ids=list(range(n_cores)))
    R = res.results
    cat = lambda name: np.concatenate([np.asarray(r[name]) for r in R], axis=0)
    y_prompt = cat("yp").reshape(B, S, D)
    y_sample = cat("ys").reshape(SBT, DEC, D)
    ncp = cat("ncp").reshape(1, B, 2, D)
    nkp = cat("nkp").reshape(1, B, S, NH, HD)
    nvp = cat("nvp").reshape(1, B, S, NH, HD)
    ncs = cat("ncs").reshape(1, SBT, 2, D)
    nks = cat("nks").reshape(1, SBT, DEC, NH, HD)
    nvs = cat("nvs").reshape(1, SBT, DEC, NH, HD)
    return tuple(np.ascontiguousarray(a, dtype=np.float32)
                 for a in (y_prompt, y_sample, ncp, nkp, nvp, ncs, nks, nvs))


def kernel(x_prompt, x_sample, cache_conv, cache_k, cache_v, g_mix, w_in, conv_w, w_out,
           g_ffn, w_up, w_down, g_final):
    return run_cores(x_prompt, x_sample, cache_conv, cache_k, cache_v, g_mix, w_in, conv_w, w_out,
                     g_ffn, w_up, w_down, g_final, N_CORES)
```
